# Optimizing a Trainium2 kernel written in Bass

```python
import math
import jax, jax.numpy as jnp
from jax import lax
import numpy as np

D_MODEL = 2048
BATCH = 4
SEQ = 2048
DEPTH = 1
DEC_BATCH = 128
DEC_SEQ = 1
PAST_LEN = 16384
PAGE_SIZE = 128

GLA_HEADS = 4
GLA_DK = 128
GLA_DV = 256
GLA_QK = GLA_HEADS * GLA_DK
GLA_V = GLA_HEADS * GLA_DV
GLA_GATE_RANK = 16
GLA_GATE_NORM = 16.0
RET_HEADS = 4
RET_DK = 256
RET_DV = 256
RET_QK = RET_HEADS * RET_DK
RET_V = RET_HEADS * RET_DV
ROPE_BASE = 10000.0
CHUNK = 64
EPS = 1e-6
D_FF = -(-8 * D_MODEL // (3 * 256)) * 256
SPLITS = (GLA_QK, GLA_QK, GLA_V, GLA_GATE_RANK, GLA_V, RET_QK, RET_QK, RET_V, RET_V, 2 * D_MODEL)
IN_WIDTH = sum(SPLITS)

kernel_name = "gla_retnet_gated_merge_decoder_step"


def rmsnorm(x, w):
    xf = x.astype(jnp.float32)
    y = xf * lax.rsqrt(jnp.mean(xf * xf, axis=-1, keepdims=True) + EPS)
    return (y * w.astype(jnp.float32)).astype(x.dtype)


def head_layernorm(o, w):
    mu = jnp.mean(o, axis=-1, keepdims=True)
    d = o - mu
    var = jnp.mean(d * d, axis=-1, keepdims=True)
    return d * lax.rsqrt(var + EPS) * w.astype(jnp.float32)


def rotary(x, pos):
    half = x.shape[-1] // 2
    inv = ROPE_BASE ** (-jnp.arange(half, dtype=jnp.float32) / half)
    ang = pos.astype(jnp.float32)[:, None] * inv[None, :]
    cos, sin = jnp.cos(ang), jnp.sin(ang)
    x1, x2 = x[..., :half], x[..., half:]
    return jnp.concatenate([x1 * cos - x2 * sin, x1 * sin + x2 * cos], axis=-1)


def chunked_linear_attention(q, k, v, g, s0, per_dim_decay):
    B, H, L, dk = q.shape
    dv = v.shape[-1]
    C = min(CHUNK, L)
    n = -(-L // C)
    pad = n * C - L
    if pad:
        padf = lambda a: jnp.pad(a, ((0, 0), (0, 0), (0, pad), (0, 0)))
        q, k, v, g = padf(q), padf(k), padf(v), padf(g)

    def to_chunks(a):
        return a.reshape(B, H, n, C, a.shape[-1]).transpose(2, 0, 1, 3, 4)

    causal = jnp.tril(jnp.ones((C, C), dtype=bool))[None, None, :, :, None]

    def step(S, inp):
        qi, ki, vi, gi = inp
        G = jnp.cumsum(gi, axis=2)
        Glast = G[:, :, -1:, :]
        o_inter = jnp.einsum('bhcd,bhde->bhce', qi * jnp.exp(G), S)
        rel = G[:, :, :, None, :] - G[:, :, None, :, :]
        decay = jnp.exp(jnp.where(causal, rel, -jnp.inf))
        if per_dim_decay:
            scores = jnp.einsum('bhid,bhjd,bhijd->bhij', qi, ki, decay)
        else:
            scores = jnp.einsum('bhid,bhjd->bhij', qi, ki) * decay[..., 0]
        o = o_inter + jnp.einsum('bhij,bhje->bhie', scores, vi)
        kd = ki * jnp.exp(Glast - G)
        S_new = jnp.exp(Glast)[:, :, 0, :, None] * S + jnp.einsum('bhcd,bhce->bhde', kd, vi)
        return S_new, o

    S_fin, oc = lax.scan(step, s0, (to_chunks(q), to_chunks(k), to_chunks(v), to_chunks(g)))
    o = oc.transpose(1, 2, 0, 3, 4).reshape(B, H, n * C, dv)[:, :, :L]
    return o, S_fin


def mixer(h, pos, s_gla, s_ret, w_in, w_gla_gate_up, b_gla_gate, gla_norm_w, w_gla_up,
          ret_norm_w, w_ret_up, w_out):
    B, L, _ = h.shape
    f32 = jnp.float32
    proj = h @ w_in
    offs = np.cumsum(SPLITS)[:-1].tolist()
    qa, ka, va, gdown, ga, qb, kb, vb, gb, mg = jnp.split(proj, offs, axis=-1)

    def heads(a, nh):
        return a.reshape(B, L, nh, -1).transpose(0, 2, 1, 3).astype(f32)

    q = heads(qa, GLA_HEADS) * (GLA_DK ** -0.5)
    k = heads(ka, GLA_HEADS)
    v = heads(va, GLA_HEADS)
    glog = jax.nn.log_sigmoid((gdown @ w_gla_gate_up + b_gla_gate).astype(f32)) / GLA_GATE_NORM
    glog = heads(glog, GLA_HEADS)
    o_a, sa = chunked_linear_attention(q, k, v, glog, s_gla.astype(f32), True)
    o_a = rmsnorm(o_a.transpose(0, 2, 1, 3), gla_norm_w).reshape(B, L, GLA_V)
    o_a = o_a * jax.nn.silu(ga.astype(f32))
    y_a = o_a.astype(h.dtype) @ w_gla_up

    qr = rotary(heads(qb, RET_HEADS), pos)
    kr = rotary(heads(kb, RET_HEADS), pos) * (RET_DK ** -0.5)
    vr = heads(vb, RET_HEADS)
    log_gamma = jnp.log1p(-jnp.exp(jnp.linspace(math.log(1.0 / 32), math.log(1.0 / 512), RET_HEADS))).astype(f32)
    gr = jnp.broadcast_to(log_gamma[None, :, None, None], (B, RET_HEADS, L, 1))
    o_b, sb = chunked_linear_attention(qr, kr, vr, gr, s_ret.astype(f32), False)
    o_b = head_layernorm(o_b.transpose(0, 2, 1, 3), ret_norm_w).reshape(B, L, RET_V)
    o_b = o_b * jax.nn.silu(gb.astype(f32))
    y_b = o_b.astype(h.dtype) @ w_ret_up

    gates = jax.nn.sigmoid(mg.astype(f32)).reshape(B, L, 2, D_MODEL)
    m = gates[:, :, 0] * y_a.astype(f32) + gates[:, :, 1] * y_b.astype(f32)
    return m.astype(h.dtype) @ w_out, sa, sb


def swiglu(h, w_gate, w_up, w_down):
    a = (h @ w_gate).astype(jnp.float32)
    b = (h @ w_up).astype(jnp.float32)
    return (jax.nn.silu(a) * b).astype(h.dtype) @ w_down


def trunk(x, pos, st_gla, st_ret, norm_mix, w_in, w_gla_gate_up, b_gla_gate, gla_norm_w, w_gla_up,
          ret_norm_w, w_ret_up, w_out, norm_ffn, w_ffn_gate, w_ffn_up, w_ffn_down, norm_final):
    new_gla, new_ret = [], []
    for l in range(DEPTH):
        h = rmsnorm(x, norm_mix[l])
        m, sa, sb = mixer(h, pos, st_gla[l], st_ret[l], w_in[l], w_gla_gate_up[l], b_gla_gate[l],
                          gla_norm_w[l], w_gla_up[l], ret_norm_w[l], w_ret_up[l], w_out[l])
        x = x + m
        x = x + swiglu(rmsnorm(x, norm_ffn[l]), w_ffn_gate[l], w_ffn_up[l], w_ffn_down[l])
        new_gla.append(sa)
        new_ret.append(sb)
    return rmsnorm(x, norm_final), jnp.stack(new_gla), jnp.stack(new_ret)


def setup_inputs(seed: int = 0) -> dict:
    key = jax.random.key(seed)
    ks = jax.random.split(key, 20)
    nrm = lambda k, shape, s: jax.random.normal(k, shape, jnp.float32) * s
    gain = lambda k, shape: 1.0 + 0.01 * jax.random.normal(k, shape, jnp.float32)
    return {
        "x_prompt": nrm(ks[0], (BATCH, SEQ, D_MODEL), 1.0),
        "x_sample": nrm(ks[1], (DEC_BATCH, DEC_SEQ, D_MODEL), 1.0),
        "state_gla": nrm(ks[2], (DEPTH, DEC_BATCH, GLA_HEADS, GLA_DK, GLA_DV), 0.5),
        "state_ret": nrm(ks[3], (DEPTH, DEC_BATCH, RET_HEADS, RET_DK, RET_DV), 0.5),
        "norm_mix": gain(ks[4], (DEPTH, D_MODEL)),
        "w_in": nrm(ks[5], (DEPTH, D_MODEL, IN_WIDTH), D_MODEL ** -0.5),
        "w_gla_gate_up": nrm(ks[6], (DEPTH, GLA_GATE_RANK, GLA_QK), GLA_GATE_RANK ** -0.5),
        "b_gla_gate": nrm(ks[7], (DEPTH, GLA_QK), 0.1),
        "gla_norm_w": gain(ks[8], (DEPTH, GLA_DV)),
        "w_gla_up": nrm(ks[9], (DEPTH, GLA_V, D_MODEL), GLA_V ** -0.5),
        "ret_norm_w": gain(ks[10], (DEPTH, RET_DV)),
        "w_ret_up": nrm(ks[11], (DEPTH, RET_V, D_MODEL), RET_V ** -0.5),
        "w_out": nrm(ks[12], (DEPTH, D_MODEL, D_MODEL), D_MODEL ** -0.5),
        "norm_ffn": gain(ks[13], (DEPTH, D_MODEL)),
        "w_ffn_gate": nrm(ks[14], (DEPTH, D_MODEL, D_FF), D_MODEL ** -0.5),
        "w_ffn_up": nrm(ks[15], (DEPTH, D_MODEL, D_FF), D_MODEL ** -0.5),
        "w_ffn_down": nrm(ks[16], (DEPTH, D_FF, D_MODEL), D_FF ** -0.5),
        "norm_final": gain(ks[17], (D_MODEL,)),
    }


def reference(x_prompt, x_sample, state_gla, state_ret, norm_mix, w_in, w_gla_gate_up, b_gla_gate,
              gla_norm_w, w_gla_up, ret_norm_w, w_ret_up, w_out, norm_ffn, w_ffn_gate, w_ffn_up,
              w_ffn_down, norm_final):
    weights = (norm_mix, w_in, w_gla_gate_up, b_gla_gate, gla_norm_w, w_gla_up, ret_norm_w, w_ret_up,
               w_out, norm_ffn, w_ffn_gate, w_ffn_up, w_ffn_down, norm_final)
    B, L, _ = x_prompt.shape
    pos_p = jnp.arange(L, dtype=jnp.int32)
    z_gla = jnp.zeros((DEPTH, B, GLA_HEADS, GLA_DK, GLA_DV), jnp.float32)
    z_ret = jnp.zeros((DEPTH, B, RET_HEADS, RET_DK, RET_DV), jnp.float32)
    y_prompt, gla_p, ret_p = trunk(x_prompt, pos_p, z_gla, z_ret, *weights)
    pos_s = PAST_LEN + jnp.arange(x_sample.shape[1], dtype=jnp.int32)
    y_sample, gla_s, ret_s = trunk(x_sample, pos_s, state_gla, state_ret, *weights)
    sd = state_gla.dtype
    return (y_prompt, y_sample, gla_p.astype(sd), ret_p.astype(state_ret.dtype), gla_s.astype(sd), ret_s.astype(state_ret.dtype))
```

```python
import contextlib
import math
import numpy as np
import concourse.bass as bass
import concourse.mybir as mybir
from concourse.bass_utils import run_bass_kernel_spmd

F32 = mybir.dt.float32
BF16 = mybir.dt.bfloat16
AF = mybir.ActivationFunctionType
ALU = mybir.AluOpType

D = 2048
KT = 16
NS = 16
T_OWN = 1040
T_PRE = 1024
DFF = 5632
QA, KA, VA, GD, GA, QB, KB, VB, GB, MG = 0, 512, 1024, 2048, 2064, 3088, 4112, 5136, 6160, 7184
IN_W = 11280
EPS = 1e-6
NCST = 1344
C_ID, C_CM, C_DQ, C_DK, C_NMIX, C_NFFN, C_BG, C_GANW, C_RENW = 0, 128, 256, 768, 1280, 1296, 1312, 1316, 1318


def _ref_constants():
    lg = np.array([-1123939604, -1135839952, -1147032826, -1157619701], dtype=np.int32).view(np.float32)
    chk = np.log1p(-np.exp(np.linspace(math.log(1.0 / 32), math.log(1.0 / 512), 4)))
    assert np.allclose(lg.astype(np.float64), chk, rtol=1e-6, atol=0.0)
    inv = (10000.0 ** (-(np.arange(128, dtype=np.float64) / 128))).astype(np.float32)
    return lg.astype(np.float64), inv


_LG, _INV = _ref_constants()
GAMMA = [float(np.exp(_LG[h])) for h in range(4)]
G128 = [float(np.exp(128.0 * _LG[h])) for h in range(4)]


class Buf:
    __slots__ = ("w", "r", "excl")

    def __init__(self, excl=False):
        self.w = None
        self.r = {}
        self.excl = excl


class Trk:
    ENG = ("pe", "act", "dve", "pool", "sp")

    def __init__(self, nc, es):
        self.nc = nc
        self.es = es
        self.sem = {}
        self.cnt = {}
        self.prog = {}
        self.waited = {}
        for e in self.ENG:
            self.sem[e] = es.enter_context(nc.semaphore("sem_" + e))
            self.cnt[e] = 0
            self.prog[e] = []
            self.waited[e] = {}
        self.dsem = {}

    def new_dsem(self, name):
        self.dsem[name] = [self.es.enter_context(self.nc.semaphore(name)), 0]
        return name

    def _h(self, key):
        return self.sem[key] if key in self.sem else self.dsem[key][0]

    def _need(self, eng, ev, waits):
        if ev is None:
            return
        k, v = ev
        if k == eng and eng == "pe":
            return
        if self.waited[eng].get(k, 0) >= v:
            return
        self.waited[eng][k] = v
        waits.append((k, v))

    def _deps(self, eng, reads, writes):
        waits = []
        for b in reads:
            self._need(eng, b.w, waits)
            if b.excl:
                for k, v in b.r.items():
                    self._need(eng, (k, v), waits)
        for b in writes:
            self._need(eng, b.w, waits)
            for k, v in b.r.items():
                self._need(eng, (k, v), waits)
        return waits

    def op(self, eng, fn, reads=(), writes=(), inc=True):
        waits = self._deps(eng, reads, writes)
        if inc:
            self.cnt[eng] += 1
            val = self.cnt[eng]
        else:
            val = self.cnt[eng] + 1
        for b in reads:
            b.r[eng] = max(b.r.get(eng, 0), val)
        for b in writes:
            b.w = (eng, val)
            b.r = {}
        self.prog[eng].append((waits, fn, inc))

    def dma(self, q, out, in_, dsem, reads=(), writes=()):
        waits = self._deps(q, reads, writes)
        d = self.dsem[dsem]
        d[1] += 16
        val = d[1]
        for b in reads:
            b.r[dsem] = max(b.r.get(dsem, 0), val)
        for b in writes:
            b.w = (dsem, val)
            b.r = {}
        h = d[0]
        self.prog[q].append((waits, lambda e, out=out, in_=in_, h=h: e.dma_start(out=out, in_=in_).then_inc(h, 16), None))

    def barrier(self, engs=("pe", "act", "dve", "sp")):
        for e in engs:
            waits = []
            for e2 in engs:
                if e2 != e and self.cnt[e2] > 0:
                    self._need(e, (e2, self.cnt[e2]), waits)
            for name, (h, c) in self.dsem.items():
                if c > 0 and not name.startswith("ring"):
                    self._need(e, (name, c), waits)
            if waits:
                self.prog[e].append((waits, None, False))

    def emit(self, block, final_waits):
        names = {"pe": "tensor", "act": "scalar", "dve": "vector", "pool": "gpsimd", "sp": "sync"}
        for e in self.ENG:
            def body(eng, e=e):
                for waits, fn, inc in self.prog[e]:
                    for k, v in waits:
                        eng.wait_ge(self._h(k), v)
                    if fn is None:
                        continue
                    ins = fn(eng)
                    if inc:
                        ins.then_inc(self.sem[e], 1)
                if e == "sp":
                    for k, v in final_waits:
                        eng.wait_ge(self._h(k), v)
            getattr(block, names[e])(body)


def build_program():
    nc = bass.Bass("TRN2", target_bir_lowering=False)

    def din(n, s):
        return nc.dram_tensor(n, s, F32, kind="ExternalInput").ap()

    def dout(n, s):
        return nc.dram_tensor(n, s, F32, kind="ExternalOutput").ap()

    xo = din("xo", [T_OWN, D]); xp = din("xp", [T_PRE, D])
    sg = din("sg", [NS, 4, 128, 256]); sr = din("sr", [NS, 4, 256, 256])
    w_in = din("w_in", [D, IN_W]); w_gup = din("w_gup", [16, 512])
    w_aup = din("w_aup", [1024, D]); w_bup = din("w_bup", [1024, D]); w_out = din("w_out", [D, D])
    w_fg = din("w_fg", [D, DFF]); w_fu = din("w_fu", [D, DFF]); w_fd = din("w_fd", [DFF, D])
    cst = din("cst", [128, NCST]); nfb = din("nfb", [128, D])
    cso = din("cso", [128, 2, T_OWN]); csp = din("csp", [128, 2, T_PRE])
    yo = dout("yo", [T_OWN, D]); gp = dout("gp", [4, 128, 256]); rp = dout("rp", [4, 256, 256])
    gs = dout("gs", [NS, 4, 128, 256]); rs = dout("rs", [NS, 4, 256, 256])

    es = contextlib.ExitStack()
    NA = 53008
    A = es.enter_context(nc.sbuf_tensor("arena", [128, NA], F32))
    AB = A.bitcast(BF16)
    ps = [es.enter_context(nc.psum_tensor("ps%d" % i, [128, 512], F32)) for i in range(8)]
    PB = [Buf(excl=True) for _ in range(8)]
    T = Trk(nc, es)

    def Fv(off, n):
        return A[:, off // 4: off // 4 + n]

    def Hv(off, n):
        return AB[:, off // 2: off // 2 + n]

    RING = [i * 8192 for i in range(4)]
    O_CST = 32768
    O_IDB = 38144
    O_WGU = 38400
    O_SML = 39424
    O_ONES = 40448
    O_GDT = 40960
    O_HT = 43072
    R0 = 76352
    O_SST = R0
    W = R0 + 12288
    O_XT = [0, 0]; O_XN = [0, 0]; O_JNK = 0
    O_QT = [W, W + 4160]; O_KT = [W + 8320, W + 12480]; O_VS = [W + 16640, W + 21248]
    O_S2 = W + 25856
    O_SCR = [W + 35072 + i * 4160 for i in range(4)]
    O_S2B = W + 51712
    O_SBF = W + 60928
    O_S0 = [W + 61952 + i * 1024 for i in range(4)]
    O_SN = [W + 66048 + i * 1024 for i in range(4)]
    O_SNB = [W + 70144 + i * 512 for i in range(4)]
    O_VM = [W + 72192 + i * 512 for i in range(2)]
    O_KST = W + 73216
    O_OSTS = W + 73728
    O_ATM = [W + 73984 + i * 256 for i in range(2)]
    O_KTK = [W + 74496 + i * 512 for i in range(2)]
    O_U = [W + 75520 + i * 2048 for i in range(2)]
    O_ON = [W + 79616 + i * 512 for i in range(2)] + [W + 88320]
    O_OSB = [W + 80640 + i * 1024 for i in range(2)] + [W + 88832]
    O_JNK2 = W + 82688
    O_TH = [W + 83712 + i * 1024 for i in range(2)]
    O_OT = W + 89856
    O_MT = W
    O_MG = W + 33280
    O_X1 = W + 33280
    O_LATE = W + 107008
    O_XT = [O_OT, O_OT + 8192, O_OT + 16384]; O_XN = [O_OT + 24576, O_OT + 28672]; O_JNK = 0
    assert O_OT + 33280 <= NA * 4 and O_LATE + 12288 <= NA * 4

    ring_modes = {}
    for n_ in (4, 8):
        sz_ = 32768 // n_
        ring_modes[n_] = ([Hv(i * sz_, sz_ // 2) for i in range(n_)], [Buf() for _ in range(n_)],
                          [T.new_dsem("ring%d_%d" % (n_, i)) for i in range(n_)])
    cur_mode = [4]
    slab_ctr = [0]

    def switch_mode(n_new):
        n_old = cur_mode[0]
        old = ring_modes[n_old][1]
        new = ring_modes[n_new][1]
        for j, nb in enumerate(new):
            lo, hi = j * 32768 // n_new, (j + 1) * 32768 // n_new
            nb.w = None
            nb.r = {}
            for i, ob in enumerate(old):
                olo, ohi = i * 32768 // n_old, (i + 1) * 32768 // n_old
                if olo < hi and lo < ohi:
                    evs = list(ob.r.items()) + ([ob.w] if ob.w is not None else [])
                    for (k, v) in evs:
                        nb.r[k] = max(nb.r.get(k, 0), v)
        cur_mode[0] = n_new
        slab_ctr[0] = 0
    CST = Fv(O_CST, NCST); B_CST = Buf()
    IDF = CST[:, C_ID:C_ID + 128]
    CMASK = CST[:, C_CM:C_CM + 128]
    IDB = Hv(O_IDB, 128); B_IDB = Buf()
    WGU = Hv(O_WGU, 512); B_WGU = Buf()
    SML = Fv(O_SML, 256); B_SML = Buf()
    ELAST = [SML[:, 96 + 8 * i: 104 + 8 * i] for i in range(3)]; B_ELAST = [Buf(), Buf(), Buf()]
    EGS = [SML[:, 128 + 16 * i: 144 + 16 * i] for i in range(3)]; B_EGS = [Buf(), Buf(), Buf()]
    ONES = Fv(O_ONES, 128); B_ONES = Buf()
    GDT = Hv(O_GDT, T_OWN); B_GDT = Buf()
    HT = Hv(O_HT, 16 * T_OWN).rearrange("p (k t) -> p k t", k=16); B_HT = Buf()
    OT = Hv(O_OT, 16 * T_OWN).rearrange("p (k t) -> p k t", k=16); B_OT = Buf()
    MT = Hv(O_MT, 16 * T_OWN).rearrange("p (k t) -> p k t", k=16); B_MT = Buf()
    SST = Fv(O_SST, 12 * 256); B_SST = [Buf() for _ in range(8)]
    NEGB = SML[:, 0:4]; EPSC = SML[:, 4:5]; LNQ = SML[:, 5:6]
    sml_rot = [0]
    sml_bufs = [Buf() for _ in range(10)]

    def sml():
        i = sml_rot[0] % 10
        sml_rot[0] += 1
        return SML[:, 16 + i * 8: 16 + i * 8 + 8], sml_bufs[i]

    misc_s = T.new_dsem("misc")
    misc2_s = T.new_dsem("misc2")
    misc3_s = T.new_dsem("misc3")
    out_s = T.new_dsem("outs")
    xs = [T.new_dsem("xs0"), T.new_dsem("xs1"), T.new_dsem("xs2")]
    cs_s = [T.new_dsem("cs0"), T.new_dsem("cs1")]
    s0_s = [T.new_dsem("s0_%d" % i) for i in range(4)]
    sn_s = [T.new_dsem("sn_%d" % i) for i in range(4)]
    x1_s = [T.new_dsem("x1_%d" % i) for i in range(9)]

    gemm_banks = [0, 1, 2]
    bank_ctr = [0]

    def gb():
        b = gemm_banks[bank_ctr[0] % len(gemm_banks)]
        bank_ctr[0] += 1
        return b

    OWN_BLOCKS = [(0, 512), (512, 1024), (1024, 1040)]
    PRE_BLOCKS = [(0, 512), (512, 1024)]
    OWN_TILES = [(i * 128, (i + 1) * 128) for i in range(8)] + [(1024, 1040)]
    PRE_TILES = [(i * 128, (i + 1) * 128) for i in range(8)]

    def load_slab(pieces, kt):
        ring_v, ring_b, ring_s = ring_modes[cur_mode[0]]
        i = slab_ctr[0] % (cur_mode[0] if cur_mode[0] == 4 else 6)
        slab_ctr[0] += 1
        ncols = sum(p[1] for p in pieces)
        view = ring_v[i][:, 0:kt * ncols].rearrange("p (k n) -> p k n", k=kt)
        for (src, n, c0) in pieces:
            T.dma("pool", view[:, :, c0:c0 + n], src.rearrange("(k p) n -> p k n", p=128), ring_s[i], writes=[ring_b[i]])
        return view, ring_b[i]

    def mm(out, lhsT, rhs, start, stop, reads, bank, inc):
        T.op("pe", lambda e, out=out, lhsT=lhsT, rhs=rhs, start=start, stop=stop: e.matmul(out, lhsT=lhsT, rhs=rhs, start=start, stop=stop),
             reads=reads, writes=[PB[bank]], inc=inc)

    def act(out, in_, func, reads, writes, bias=None, scale=None, accum=None):
        kw = {}
        if bias is not None:
            kw["bias"] = bias
        if scale is not None:
            kw["scale"] = scale
        if accum is not None:
            kw["accum_out"] = accum
        T.op("act", lambda e, out=out, in_=in_, func=func, kw=kw: e.activation(out=out, in_=in_, func=func, **kw), reads=reads, writes=writes)

    def tt(out, in0, in1, op, reads, writes):
        T.op("dve", lambda e, out=out, in0=in0, in1=in1, op=op: e.tensor_tensor(out=out, in0=in0, in1=in1, op=op), reads=reads, writes=writes)

    def stt(out, in0, scalar, in1, op0, op1, reads, writes):
        T.op("dve", lambda e, out=out, in0=in0, scalar=scalar, in1=in1, op0=op0, op1=op1: e.scalar_tensor_tensor(out=out, in0=in0, scalar=scalar, in1=in1, op0=op0, op1=op1),
             reads=reads, writes=writes)

    def tsmul(out, in0, scalar, reads, writes):
        T.op("dve", lambda e, out=out, in0=in0, scalar=scalar: e.tensor_scalar(out=out, in0=in0, scalar1=scalar, scalar2=0.0, op0=ALU.mult, op1=ALU.add),
             reads=reads, writes=writes)

    def memset(ap, val, writes):
        T.op("dve", lambda e, ap=ap, val=val: e.memset(ap, val), writes=writes)

    def rstd_from_ss(ss_ap, n_feat, sbuf, n):
        sc, b = sml()
        act(sc[0:n, 0:1], ss_ap, AF.Ln, reads=[sbuf, B_SML], writes=[b], scale=1.0 / n_feat, bias=EPSC[0:n, :])
        act(sc[0:n, 1:2], sc[0:n, 0:1], AF.Exp, reads=[b], writes=[b], scale=-0.5)
        return sc[0:n, 1:2], b

    T.dma("sp", CST, cst, misc_s, writes=[B_CST])
    T.dma("pool", WGU[0:16, :], w_gup, misc2_s, writes=[B_WGU])
    T.op("dve", lambda e: e.tensor_copy(out=IDB, in_=IDF), reads=[B_CST], writes=[B_IDB])
    memset(ONES, 1.0, [B_ONES])
    tsmul(NEGB, CST[:, C_BG:C_BG + 4], -1.0, [B_CST], [B_SML])
    memset(EPSC, EPS, [B_SML])
    memset(LNQ, math.log(128.0 ** -0.5), [B_SML])

    p0_bufs = []
    P0B = [Buf() for _ in range(5)]

    def norm_transpose(src_dram, tiles, nwc, from_x1=None, only_setup=False, ticker=None):
        XT = [Fv(O_XT[i], D) for i in range(3)]
        XN = [Hv(O_XN[i] if from_x1 is None else O_LATE + 8192 + i * 4096, D) for i in range(2)]
        if from_x1 is None:
            b_xt, b_xn = P0B[0:3], P0B[3:5]
            p0_bufs[:] = P0B
        else:
            b_xt = [Buf(), Buf(), Buf()]; b_xn = B_XN2

        def do_tile(ti):
            t0, t1 = tiles[ti]
            n = t1 - t0
            k = ti % 2
            if from_x1 is None:
                k3 = ti % 3
                T.dma("sp", XT[k3][0:n, :], src_dram[t0:t1, :], xs[k3], writes=[b_xt[k3]])
                xin, bx = XT[k3], b_xt[k3]
            else:
                xin, bx = from_x1[ti]
            sc, b = sml()
            act(XN[k][0:n, :], xin[0:n, :], AF.Square, reads=[bx], writes=[b_xn[k], b], accum=sc[0:n, 2:3])
            r, rb = rstd_from_ss(sc[0:n, 2:3], D, b, n)
            tsmul(XN[k][0:n, :], xin[0:n, :], r, [bx, rb], [b_xn[k]])
            for g in range(4):
                bk = gb()
                for j in range(4):
                    kt = g * 4 + j
                    mm(ps[bk][:, j * 128: j * 128 + n], XN[k][0:n, kt * 128:(kt + 1) * 128], IDB[0:n, 0:n], True, True,
                       [b_xn[k], B_IDB], bk, j == 3)
                tt(HT[:, g * 4:(g + 1) * 4, t0:t1], ps[bk][:, :].rearrange("p (a b) -> p a b", a=4)[:, :, 0:n],
                   CST[:, nwc + g * 4: nwc + (g + 1) * 4].unsqueeze(2).broadcast_to([128, 4, n]), ALU.mult,
                   [PB[bk], B_CST], [B_HT])
                if ticker is not None:
                    ticker()
        if only_setup:
            return do_tile
        for ti in range(len(tiles)):
            do_tile(ti)

    B_XN2 = [Buf(), Buf()]

    QT = [Hv(O_QT[i], 2 * T_OWN).rearrange("p (d t) -> p d t", d=2) for i in range(2)]
    KTt = [Hv(O_KT[i], 2 * T_OWN).rearrange("p (d t) -> p d t", d=2) for i in range(2)]
    VS = [Hv(O_VS[i], 9 * 256).rearrange("p (c e) -> p c e", c=9) for i in range(2)]
    S2 = [Fv(o, 9 * 256).rearrange("p (c e) -> p c e", c=9) for o in (O_S2, O_S2B)]
    SCR = [Fv(O_SCR[i], T_OWN) for i in range(4)]
    B_QT = [Buf(), Buf()]; B_KT = [Buf(), Buf()]; B_VS = [Buf(), Buf()]; B_S2 = [Buf(), Buf()]
    B_SCR = [Buf() for _ in range(4)]
    SBF = Hv(O_SBF, 512); B_SBF = Buf()
    S0 = [Fv(O_S0[i], 256) for i in range(4)]; B_S0 = [Buf() for _ in range(4)]
    SN = [Fv(O_SN[i], 256) for i in range(4)]; B_SN = [Buf() for _ in range(4)]
    SNB = [Hv(O_SNB[i], 256) for i in range(4)]; B_SNB = [Buf() for _ in range(4)]
    VM = [Hv(O_VM[i], 256) for i in range(2)]; B_VM = [Buf(), Buf()]
    KST = Hv(O_KST, 256); B_KST = Buf()
    OSTS = Fv(O_OSTS, 32); B_OSTS = Buf()
    ATM = [Hv(O_ATM[i], 128) for i in range(2)]; B_ATM = [Buf(), Buf()]
    KTK = [Hv(O_KTK[i], 256) for i in range(2)]; B_KTK = [Buf(), Buf()]
    U = [Fv(O_U[i], 512) for i in range(2)]; B_U = [Buf(), Buf()]
    ON = [Hv(O_ON[i], 256) for i in range(3)]; B_ON = [Buf(), Buf(), Buf()]
    OSB = [Fv(O_OSB[i], 256) for i in range(3)]; B_OSB = [Buf(), Buf(), Buf()]
    JNK2 = Hv(O_JNK2, 256); B_JNK2 = Buf()
    TH = [Fv(O_TH[i], 256) for i in range(2)]; B_TH = [Buf(), Buf()]
    rot = {"atm": 0, "ktk": 0, "u": 0, "on": 0, "osb": 0, "th": 0, "vm": 0, "s0": 0, "snb": 0}

    def nxt(k, m=2):
        i = rot[k] % m
        rot[k] += 1
        return i

    def sst_view(br, h):
        nd = 1 if br == 0 else 2
        off = h * 256 if br == 0 else 1024 + h * 512
        return SST[:, off: off + nd * 256], B_SST[br * 4 + h]

    def post_a(br, o_ps, o_bank, n, ti, s2b, dedicated=False):
        i = 2 if dedicated else nxt("on")
        if br == 0:
            sc, b = sml()
            act(JNK2[0:n, :], o_ps, AF.Square, reads=[PB[o_bank]], writes=[B_JNK2, b], accum=sc[0:n, 2:3])
            r, rb = rstd_from_ss(sc[0:n, 2:3], 256, b, n)
            stt(ON[i][0:n, :], o_ps, r, S2[s2b][0:n, ti, :], ALU.mult, ALU.mult, [PB[o_bank], rb, B_S2[s2b]], [B_ON[i]])
        else:
            j = 2 if dedicated else nxt("osb")
            sc, b = sml()
            act(OSB[j][0:n, :], o_ps, AF.Identity, reads=[PB[o_bank]], writes=[B_OSB[j], b], accum=sc[0:n, 2:3])
            tsmul(sc[0:n, 3:4], sc[0:n, 2:3], -1.0 / 256, [b], [b])
            act(JNK2[0:n, :], OSB[j][0:n, :], AF.Square, reads=[B_OSB[j], b], writes=[B_JNK2, b], bias=sc[0:n, 3:4], accum=sc[0:n, 4:5])
            r, rb = rstd_from_ss(sc[0:n, 4:5], 256, b, n)
            stt(OSB[j][0:n, :], OSB[j][0:n, :], sc[0:n, 3:4], S2[s2b][0:n, ti, :], ALU.add, ALU.mult, [B_OSB[j], b, B_S2[s2b]], [B_OSB[j]])
            tsmul(ON[i][0:n, :], OSB[j][0:n, :], r, [B_OSB[j], rb], [B_ON[i]])
        return i

    def post_b(br, h, i, n, t0, t1):
        nwc = C_GANW if br == 0 else C_RENW
        for ec in range(2):
            mm(ps[5][:, ec * 128: ec * 128 + n], ON[i][0:n, ec * 128:(ec + 1) * 128], IDB[0:n, 0:n], True, True,
               [B_ON[i], B_IDB], 5, ec == 1)
        k0 = br * 8 + h * 2
        tt(OT[:, k0:k0 + 2, t0:t1], ps[5][:, 0:256].rearrange("p (a b) -> p a b", a=2)[:, :, 0:n],
           CST[:, nwc:nwc + 2].unsqueeze(2).broadcast_to([128, 2, n]), ALU.mult, [PB[5], B_CST], [B_OT] + p0_bufs)
        del p0_bufs[:]

    def attention(br, h, hb, prefix, elast_ap, elast_buf):
        nd = 1 if br == 0 else 2
        S, bS = sst_view(br, h)
        if prefix:
            memset(S, 0.0, [bS])
        else:
            act(SBF[:, 0:nd * 256], S, AF.Copy, reads=[bS], writes=[B_SBF])
        pend_b = None
        for c in range(8):
            ch0, ch1 = c * 128, (c + 1) * 128
            if not prefix:
                for dt in range(nd):
                    mm(ps[3][:, 0:128], KTt[hb][:, dt, ch0:ch1], QT[hb][:, dt, ch0:ch1], dt == 0, dt == nd - 1,
                       [B_KT[hb], B_QT[hb]], 3, dt == nd - 1)
                ia = nxt("atm")
                tt(ATM[ia], ps[3][:, 0:128], CMASK, ALU.mult, [PB[3], B_CST], [B_ATM[ia]])
                if pend_b is not None:
                    post_b(*pend_b)
                    pend_b = None
                yield
                mm(ps[4][:, 0:256], ATM[ia], VS[hb][:, c, :], True, False, [B_ATM[ia], B_VS[hb]], 4, False)
                for dt in range(nd):
                    mm(ps[4][:, 0:256], QT[hb][:, dt, ch0:ch1], SBF[:, dt * 256:(dt + 1) * 256], False, dt == nd - 1,
                       [B_QT[hb], B_SBF], 4, dt == nd - 1)
            for dt in range(nd):
                mm(ps[3][:, 128 + dt * 128: 256 + dt * 128], KTt[hb][:, dt, ch0:ch1], IDB, True, True, [B_KT[hb], B_IDB], 3, dt == nd - 1)
            ik = nxt("ktk")
            if br == 0:
                act(KTK[ik][:, 0:nd * 128], ps[3][:, 128:128 + nd * 128], AF.Copy, reads=[PB[3]], writes=[B_KTK[ik]])
            else:
                act(KTK[ik][:, 0:nd * 128], ps[3][:, 128:128 + nd * 128], AF.Copy, reads=[PB[3]], writes=[B_KTK[ik]], scale=G128[h])
            if not prefix:
                io = post_a(br, ps[4][:, 0:256], 4, 128, c, hb)
                pend_b = (br, h, io, 128, ch0, ch1)
            yield
            for dt in range(nd):
                mm(ps[6][:, dt * 256:(dt + 1) * 256], KTK[ik][:, dt * 128:(dt + 1) * 128], VS[hb][:, c, :], True, True,
                   [B_KTK[ik], B_VS[hb]], 6, dt == nd - 1)
            if br == 0:
                j = nxt("u")
                el = elast_ap[:, c:c + 1]
                rd = [elast_buf]
                act(U[j][:, 0:nd * 256], ps[6][:, 0:nd * 256], AF.Copy, reads=[PB[6]] + rd, writes=[B_U[j]], scale=el)
                stt(S, S, el, U[j][:, 0:nd * 256], ALU.mult, ALU.add, [bS, B_U[j]] + rd, [bS])
            else:
                stt(S, S, G128[h], ps[6][:, 0:nd * 256], ALU.mult, ALU.add, [bS, PB[6]], [bS])
            if not prefix:
                act(SBF[:, 0:nd * 256], S, AF.Copy, reads=[bS], writes=[B_SBF])
            yield
        if pend_b is not None:
            post_b(*pend_b)
            yield
        if not prefix:
            if br == 0:
                T.dma("sp", gp[h], S, out_s, reads=[bS])
            else:
                T.dma("sp", rp[h].rearrange("(d p) e -> p d e", p=128), S.rearrange("p (d e) -> p d e", d=2), out_s, reads=[bS])

    def sample_step(br, h, hb, egs_ap, egs_buf):
        nd = 1 if br == 0 else 2
        for dt in range(nd):
            mm(ps[7][0:16, dt * 128:(dt + 1) * 128], KTt[hb][:, dt, 1024:1040], IDB, True, True, [B_KT[hb], B_IDB], 7, dt == nd - 1)
        act(KST[0:16, 0:nd * 128], ps[7][0:16, 0:nd * 128], AF.Copy, reads=[PB[7]], writes=[B_KST])
        yield

        def os_mm(t, snb_idx):
            for ec in range(2):
                for dt in range(nd):
                    k = snb_idx[dt]
                    mm(ps[5][:, 256 + ec * 16 + t: 256 + ec * 16 + t + 1], SNB[k][:, ec * 128:(ec + 1) * 128],
                       QT[hb][:, dt, 1024 + t: 1025 + t], dt == 0, dt == nd - 1, [B_SNB[k], B_QT[hb]], 5, dt == nd - 1)
        prev = None
        iv_next = nxt("vm")
        tsmul(VM[iv_next][0:16, :], VS[hb][0:16, 8, :], IDF[0:16, 0:1], [B_VS[hb], B_CST], [B_VM[iv_next]])
        yield
        for t in range(NS):
            if prev is not None:
                os_mm(*prev)
            iv = iv_next
            js = []
            for dt in range(nd):
                j = nxt("s0", 4)
                src = sg[t, h] if br == 0 else sr[t, h, dt * 128:(dt + 1) * 128, :]
                T.dma("sp", S0[j], src, s0_s[j], writes=[B_S0[j]])
                mm(ps[7][:, dt * 256:(dt + 1) * 256], KST[0:16, dt * 128:(dt + 1) * 128], VM[iv][0:16, :], True, True, [B_KST, B_VM[iv]], 7, dt == nd - 1)
                js.append(j)
            if t + 1 < NS:
                iv_next = nxt("vm")
                tsmul(VM[iv_next][0:16, :], VS[hb][0:16, 8, :], IDF[0:16, t + 1:t + 2], [B_VS[hb], B_CST], [B_VM[iv_next]])
            snb_idx = []
            for dt in range(nd):
                j = js[dt]
                dst = gs[t, h] if br == 0 else rs[t, h, dt * 128:(dt + 1) * 128, :]
                if br == 0:
                    el = egs_ap[:, t:t + 1]; rd = [egs_buf]
                else:
                    el = GAMMA[h]; rd = []
                stt(SN[j], S0[j], el, ps[7][:, dt * 256:(dt + 1) * 256], ALU.mult, ALU.add, [B_S0[j], PB[7]] + rd, [B_SN[j]])
                k = nxt("snb", 4)
                act(SNB[k], SN[j], AF.Copy, reads=[B_SN[j]], writes=[B_SNB[k]])
                T.dma("act", dst, SN[j], sn_s[j], reads=[B_SN[j]])
                snb_idx.append(k)
            prev = (t, snb_idx)
            yield
        os_mm(*prev)
        act(OSTS, ps[5][:, 256:288], AF.Copy, reads=[PB[5]], writes=[B_OSTS])
        yield
        for ec in range(2):
            mm(ps[4][0:16, ec * 128:(ec + 1) * 128], OSTS[:, ec * 16:(ec + 1) * 16], IDF, True, True, [B_OSTS, B_CST], 4, ec == 1)
        io = post_a(br, ps[4][0:16, 0:256], 4, 16, 8, hb, dedicated=True)
        yield
        post_b(br, h, io, 16, 1024, 1040)
        yield

    pending = []

    rr = [0]

    def tick():
        while pending:
            i = rr[0] % len(pending)
            rr[0] += 1
            try:
                next(pending[i])
                return
            except StopIteration:
                pending.pop(i)

    def drain():
        while pending:
            tick()

    def proj_fm(slab, sbuf, c0, blocks, evac):
        for (t0, t1) in blocks:
            n = t1 - t0
            bk = gb()
            for kt in range(KT):
                mm(ps[bk][:, 0:n], slab[:, kt, c0:c0 + 128], HT[:, kt, t0:t1], kt == 0, kt == KT - 1, [sbuf, B_HT], bk, kt == KT - 1)
                if kt == 7 and n > 16:
                    tick()
            evac(ps[bk][:, 0:n], bk, t0, t1)
            tick()

    def proj_tm(slab, sbuf, tiles, evac):
        for ti, (t0, t1) in enumerate(tiles):
            n = t1 - t0
            bk = gb()
            for kt in range(KT):
                mm(ps[bk][0:n, 0:256], HT[:, kt, t0:t1], slab[:, kt, 0:256], kt == 0, kt == KT - 1, [B_HT, sbuf], bk, kt == KT - 1)
                if kt == 7 and n > 16:
                    tick()
            evac(ps[bk][0:n, 0:256], bk, ti, n)
            tick()

    def silu_evac(dst_fn, dbuf):
        def ev(p, bk, ti, n):
            i = nxt("th")
            act(TH[i][0:n, :], p, AF.Exp, reads=[PB[bk]], writes=[B_TH[i]], scale=-1.0)
            act(TH[i][0:n, :], TH[i][0:n, :], AF.Ln, reads=[B_TH[i]], writes=[B_TH[i]], bias=1.0)
            act(TH[i][0:n, :], TH[i][0:n, :], AF.Exp, reads=[B_TH[i]], writes=[B_TH[i]], scale=-1.0)
            tt(dst_fn(ti, n), p, TH[i][0:n, :], ALU.mult, [PB[bk], B_TH[i]], [dbuf])
        return ev

    def win_pass(prefix, final_drain=True):
        blocks = PRE_BLOCKS if prefix else OWN_BLOCKS
        tiles = PRE_TILES if prefix else OWN_TILES
        csd = csp if prefix else cso
        ntk = blocks[-1][1]
        slab, sb = load_slab([(w_in[:, GD:GD + 16], 16, 0)], KT)
        for (t0, t1) in blocks:
            n = t1 - t0
            bk = gb()
            for kt in range(KT):
                mm(ps[bk][0:16, 0:n], slab[:, kt, 0:16], HT[:, kt, t0:t1], kt == 0, kt == KT - 1, [sb, B_HT], bk, kt == KT - 1)
            act(GDT[0:16, t0:t1], ps[bk][0:16, 0:n], AF.Copy, reads=[PB[bk]], writes=[B_GDT])

        def e_stage(h):
            hb = h % 3
            for (t0, t1) in blocks:
                n = t1 - t0
                bk = gb()
                mm(ps[bk][:, 0:n], WGU[0:16, h * 128:(h + 1) * 128], GDT[0:16, t0:t1], True, True, [B_WGU, B_GDT], bk, True)
                act(SCR[0][:, t0:t1], ps[bk][:, 0:n], AF.Exp, reads=[PB[bk], B_SML], writes=[B_SCR[0]], scale=-1.0, bias=NEGB[:, h:h + 1])
                tick()
            act(SCR[0][:, 0:ntk], SCR[0][:, 0:ntk], AF.Ln, reads=[B_SCR[0]], writes=[B_SCR[0]], bias=1.0)
            for c in range(8):
                T.op("dve", lambda e, c=c: e.tensor_tensor_scan(out=SCR[1][:, c * 128:(c + 1) * 128], data0=ONES[:, 0:128],
                                                                  data1=SCR[0][:, c * 128:(c + 1) * 128], initial=0.0, op0=ALU.mult, op1=ALU.add),
                     reads=[B_SCR[0], B_ONES], writes=[B_SCR[1]])
            if not prefix:
                act(SCR[2][:, 0:1024], SCR[1][:, 0:1024], AF.Exp, reads=[B_SCR[1], B_SML], writes=[B_SCR[2]], scale=-1.0 / 16, bias=LNQ)
                memset(SCR[2][:, 1024:1040], 128.0 ** -0.5, [B_SCR[2]])
            act(SCR[3][:, 0:1024], SCR[1][:, 0:1024], AF.Exp, reads=[B_SCR[1]], writes=[B_SCR[3]], scale=1.0 / 16)
            if not prefix:
                memset(SCR[3][:, 1024:1040], 1.0, [B_SCR[3]])
            act(ELAST[hb], SCR[1][:, 127:1024:128], AF.Exp, reads=[B_SCR[1]], writes=[B_ELAST[hb]], scale=-1.0 / 16)
            if not prefix:
                act(EGS[hb], SCR[0][:, 1024:1040], AF.Exp, reads=[B_SCR[0]], writes=[B_EGS[hb]], scale=-1.0 / 16)

        hcount = 0
        for br in (1, 0):
            for h in range(4):
                hb = hcount % 2
                hcount += 1
                elast = egs = None
                ebuf = None
                if br == 0:
                    elast, ebuf = ELAST[h % 3], B_ELAST[h % 3]
                    egs = EGS[h % 3]
                    slab, sb = load_slab([(w_in[:, QA + h * 128: QA + (h + 1) * 128], 128, 0),
                                          (w_in[:, KA + h * 128: KA + (h + 1) * 128], 128, 128)], KT)
                    if not prefix:
                        proj_fm(slab, sb, 0, blocks, lambda p, bk, t0, t1: tt(QT[hb][:, 0, t0:t1], p, SCR[2][:, t0:t1], ALU.mult,
                                                                             [PB[bk], B_SCR[2]], [B_QT[hb]]))
                    proj_fm(slab, sb, 128, blocks, lambda p, bk, t0, t1: tt(KTt[hb][:, 0, t0:t1], p, SCR[3][:, t0:t1], ALU.mult,
                                                                           [PB[bk], B_SCR[3]], [B_KT[hb]]))
                    if h < 3:
                        e_stage(h + 1)
                    vcol, gcol = VA + h * 256, GA + h * 256
                else:
                    def rot_block(slab, sb, t0, t1, dec_col, scale_s, dst, dbuf):
                        n = t1 - t0
                        cs = SCR[3]
                        T.dma("sp", cs[:, 0:1040].rearrange("p (a t) -> p a t", a=2)[:, :, 0:n], csd[:, :, t0:t1], cs_s[0], writes=[B_SCR[3]])
                        cosb, sinb = cs[:, 0:n], cs[:, 520:520 + n]
                        ctab, stab = SCR[0][:, 0:n], SCR[0][:, 520:520 + n]
                        if n == 512:
                            dec = CST[:, dec_col + h * 128: dec_col + (h + 1) * 128].unsqueeze(1).broadcast_to([128, 4, 128])
                            tt(ctab.rearrange("p (a b) -> p a b", a=4), cosb.rearrange("p (a b) -> p a b", a=4), dec, ALU.mult,
                               [B_SCR[3], B_CST], [B_SCR[0]])
                            tt(stab.rearrange("p (a b) -> p a b", a=4), sinb.rearrange("p (a b) -> p a b", a=4), dec, ALU.mult,
                               [B_SCR[3], B_CST], [B_SCR[0]])
                        else:
                            tsmul(ctab, cosb, scale_s, [B_SCR[3]], [B_SCR[0]])
                            tsmul(stab, sinb, scale_s, [B_SCR[3]], [B_SCR[0]])
                        b1, b2 = gb(), gb()
                        for dt, bk in ((0, b1), (1, b2)):
                            for kt in range(KT):
                                mm(ps[bk][:, 0:n], slab[:, kt, dt * 128:(dt + 1) * 128], HT[:, kt, t0:t1], kt == 0, kt == KT - 1,
                                   [sb, B_HT], bk, kt == KT - 1)
                                if kt == 7 and n > 16:
                                    tick()
                        x1, x2 = ps[b1][:, 0:n], ps[b2][:, 0:n]
                        ta, tb = SCR[1][:, 0:n], SCR[1][:, 520:520 + n]
                        tc, td = SCR[2][:, 0:n], SCR[2][:, 520:520 + n]
                        tt(ta, x1, ctab, ALU.mult, [PB[b1], B_SCR[0]], [B_SCR[1]])
                        tt(tb, x2, stab, ALU.mult, [PB[b2], B_SCR[0]], [B_SCR[1]])
                        tt(tc, x1, stab, ALU.mult, [PB[b1], B_SCR[0]], [B_SCR[2]])
                        tt(td, x2, ctab, ALU.mult, [PB[b2], B_SCR[0]], [B_SCR[2]])
                        tt(dst[:, 0, t0:t1], ta, tb, ALU.subtract, [B_SCR[1]], [dbuf])
                        tt(dst[:, 1, t0:t1], tc, td, ALU.add, [B_SCR[2]], [dbuf])
                        tick()
                    if not prefix:
                        slab, sb = load_slab([(w_in[:, QB + h * 256: QB + (h + 1) * 256], 256, 0)], KT)
                        for (t0, t1) in blocks:
                            rot_block(slab, sb, t0, t1, C_DQ, 1.0, QT[hb], B_QT[hb])
                    slab, sb = load_slab([(w_in[:, KB + h * 256: KB + (h + 1) * 256], 256, 0)], KT)
                    for (t0, t1) in blocks:
                        rot_block(slab, sb, t0, t1, C_DK, 1.0 / 16, KTt[hb], B_KT[hb])
                    if h == 3:
                        e_stage(0)
                    vcol, gcol = VB + h * 256, GB + h * 256
                slab, sb = load_slab([(w_in[:, vcol: vcol + 256], 256, 0)], KT)
                proj_tm(slab, sb, tiles, lambda p, bk, ti, n: act(VS[hb][0:n, ti, :], p, AF.Copy, reads=[PB[bk]], writes=[B_VS[hb]]))
                if not prefix:
                    slab, sb = load_slab([(w_in[:, gcol: gcol + 256], 256, 0)], KT)
                    proj_tm(slab, sb, tiles, silu_evac(lambda ti, n: S2[hb][0:n, ti, :], B_S2[hb]))
                drain()
                pending.append(attention(br, h, hb, prefix, elast, ebuf))
                if not prefix:
                    pending.append(sample_step(br, h, hb, egs, B_EGS[h % 3]))
        if final_drain:
            drain()

    def barrier_note():
        pass

    norm_transpose(xp, PRE_TILES, C_NMIX)
    win_pass(True, final_drain=False)
    norm_transpose(xo, OWN_TILES, C_NMIX, ticker=lambda: tick())
    drain()
    win_pass(False)
    T.barrier()

    gemm_banks[:] = [0, 1, 2, 3, 4, 5, 6, 7]
    MGT = [Fv(O_MG + i * 2048, 512) for i in range(5)]
    B_MGT = [Buf() for _ in range(5)]

    def sigmoid_to(dst, dbuf, p, bk, n):
        act(dst[:, 0:n], p, AF.Exp, reads=[PB[bk]], writes=[dbuf], scale=-1.0)
        act(dst[:, 0:n], dst[:, 0:n], AF.Ln, reads=[dbuf], writes=[dbuf], bias=1.0)
        act(dst[:, 0:n], dst[:, 0:n], AF.Exp, reads=[dbuf], writes=[dbuf], scale=-1.0)

    X1 = [Fv(O_X1 + i * 8192, D) for i in range(9)]
    B_X1 = [Buf() for _ in range(9)]
    X1_EARLY = (2, 3, 4, 5)
    assert O_X1 + 2 * 8192 >= O_MG + 5 * 2048 and O_X1 + 6 * 8192 <= O_OT
    for ti in X1_EARLY:
        t0, t1 = OWN_TILES[ti]
        T.dma("sp", X1[ti][0:t1 - t0, :], xo[t0:t1, :], x1_s[ti], writes=[B_X1[ti]])
    switch_mode(8)
    for nt in range(16):
        c0 = nt * 128
        sab, sabb = load_slab([(w_aup[:, c0:c0 + 128], 128, 0), (w_bup[:, c0:c0 + 128], 128, 128)], 8)
        sga, sgab = load_slab([(w_in[:, MG + c0: MG + c0 + 128], 128, 0)], KT)
        sgb, sgbb = load_slab([(w_in[:, MG + D + c0: MG + D + c0 + 128], 128, 0)], KT)
        for (t0, t1) in OWN_BLOCKS:
            n = t1 - t0
            ba, bb, bga, bgb = gb(), gb(), gb(), gb()
            for kt in range(KT):
                mm(ps[bga][:, 0:n], sga[:, kt, :], HT[:, kt, t0:t1], kt == 0, kt == KT - 1, [sgab, B_HT], bga, kt == KT - 1)
            for kt in range(KT):
                mm(ps[bgb][:, 0:n], sgb[:, kt, :], HT[:, kt, t0:t1], kt == 0, kt == KT - 1, [sgbb, B_HT], bgb, kt == KT - 1)
            for kt in range(8):
                mm(ps[ba][:, 0:n], sab[:, kt, 0:128], OT[:, kt, t0:t1], kt == 0, kt == 7, [sabb, B_OT], ba, kt == 7)
            for kt in range(8):
                mm(ps[bb][:, 0:n], sab[:, kt, 128:256], OT[:, 8 + kt, t0:t1], kt == 0, kt == 7, [sabb, B_OT], bb, kt == 7)
            sigmoid_to(MGT[0], B_MGT[0], ps[bga][:, 0:n], bga, n)
            sigmoid_to(MGT[1], B_MGT[1], ps[bgb][:, 0:n], bgb, n)
            tt(MGT[2][:, 0:n], ps[ba][:, 0:n], MGT[0][:, 0:n], ALU.mult, [PB[ba], B_MGT[0]], [B_MGT[2]])
            tt(MGT[3][:, 0:n], ps[bb][:, 0:n], MGT[1][:, 0:n], ALU.mult, [PB[bb], B_MGT[1]], [B_MGT[3]])
            tt(MT[:, nt, t0:t1], MGT[2][:, 0:n], MGT[3][:, 0:n], ALU.add, [B_MGT[2], B_MGT[3]], [B_MT])
    switch_mode(4)

    T.barrier()
    for ti, (t0, t1) in enumerate(OWN_TILES):
        if ti not in X1_EARLY:
            T.dma("sp", X1[ti][0:t1 - t0, :], xo[t0:t1, :], x1_s[ti], writes=[B_X1[ti]])
    def wout_group(slab, sb, cg, ti):
        t0, t1 = OWN_TILES[ti]
        n = t1 - t0
        bk = gb()
        for kt in range(KT):
            mm(ps[bk][0:n, 0:256], MT[:, kt, t0:t1], slab[:, kt, :], kt == 0, kt == KT - 1, [B_MT, sb], bk, kt == KT - 1)
        xs_ = X1[ti][0:n, cg * 256:(cg + 1) * 256]
        tt(xs_, xs_, ps[bk][0:n, 0:256], ALU.add, [PB[bk], B_X1[ti]], [B_X1[ti]])

    for cg in range(4):
        slab, sb = load_slab([(w_out[:, cg * 256:(cg + 1) * 256], 256, 0)], KT)
        for ti in (2, 3, 4, 5, 0, 1, 6, 7, 8):
            wout_group(slab, sb, cg, ti)
    slabs2 = [load_slab([(w_out[:, cg * 256:(cg + 1) * 256], 256, 0)], KT) for cg in range(4, 8)]
    norm2_tile = norm_transpose(None, OWN_TILES, C_NFFN, from_x1=[(X1[i], B_X1[i]) for i in range(9)], only_setup=True)
    for ti in range(9):
        for j, cg in enumerate(range(4, 8)):
            wout_group(slabs2[j][0], slabs2[j][1], cg, ti)
        if ti >= 1:
            norm2_tile(ti - 1)
    norm2_tile(8)

    ACTT = MT
    B_ACTT = B_MT
    FT = [Fv(O_LATE + i * 2048, 512) for i in range(4)]
    B_FT = [Buf() for _ in range(4)]
    fr = [0]
    for blk in range(4):
        r0 = blk * 1408
        for j in range(6):
            ncols = 256 if j < 5 else 128
            c0 = r0 + j * 256
            sgt, sgtb = load_slab([(w_fg[:, c0:c0 + ncols], ncols, 0)], KT)
            sut, sutb = load_slab([(w_fu[:, c0:c0 + ncols], ncols, 0)], KT)
            for q in range(ncols // 128):
                jt = j * 2 + q
                for (t0, t1) in OWN_BLOCKS:
                    n = t1 - t0
                    ba, bb = gb(), gb()
                    for kt in range(KT):
                        mm(ps[ba][:, 0:n], sgt[:, kt, q * 128:(q + 1) * 128], HT[:, kt, t0:t1], kt == 0, kt == KT - 1, [sgtb, B_HT], ba, kt == KT - 1)
                    for kt in range(KT):
                        mm(ps[bb][:, 0:n], sut[:, kt, q * 128:(q + 1) * 128], HT[:, kt, t0:t1], kt == 0, kt == KT - 1, [sutb, B_HT], bb, kt == KT - 1)
                    i = fr[0] % 2
                    fr[0] += 1
                    sigmoid_to(FT[i], B_FT[i], ps[ba][:, 0:n], ba, n)
                    tt(FT[2 + i][:, 0:n], ps[ba][:, 0:n], FT[i][:, 0:n], ALU.mult, [PB[ba], B_FT[i]], [B_FT[2 + i]])
                    tt(ACTT[:, jt, t0:t1], FT[2 + i][:, 0:n], ps[bb][:, 0:n], ALU.mult, [B_FT[2 + i], PB[bb]], [B_ACTT])
        def down_group(slab, sb, cg, ti):
            t0, t1 = OWN_TILES[ti]
            n = t1 - t0
            bk = gb()
            for kt in range(11):
                mm(ps[bk][0:n, 0:256], ACTT[:, kt, t0:t1], slab[:, kt, :], kt == 0, kt == 10, [B_ACTT, sb], bk, kt == 10)
            xs_ = X1[ti][0:n, cg * 256:(cg + 1) * 256]
            tt(xs_, xs_, ps[bk][0:n, 0:256], ALU.add, [PB[bk], B_X1[ti]], [B_X1[ti]])

        for cg in range(8 if blk < 3 else 4):
            slab, sb = load_slab([(w_fd[r0:r0 + 1408, cg * 256:(cg + 1) * 256], 256, 0)], 11)
            for ti in range(9):
                down_group(slab, sb, cg, ti)

    r0 = 3 * 1408
    slabs3 = [load_slab([(w_fd[r0:r0 + 1408, cg * 256:(cg + 1) * 256], 256, 0)], 11) for cg in range(4, 8)]
    NFB = Fv(O_LATE, D); B_NFB = Buf()
    T.dma("sp", NFB, nfb, misc3_s, reads=B_FT, writes=[B_NFB] + B_FT)
    JN = Hv(O_LATE + 8192, D)

    def final_tile(ti):
        t0, t1 = OWN_TILES[ti]
        n = t1 - t0
        sc, b = sml()
        act(JN[0:n, :], X1[ti][0:n, :], AF.Square, reads=[B_X1[ti]], writes=[B_XN2[0], b], accum=sc[0:n, 2:3])
        r, rb = rstd_from_ss(sc[0:n, 2:3], D, b, n)
        stt(X1[ti][0:n, :], X1[ti][0:n, :], r, NFB[0:n, :], ALU.mult, ALU.mult, [B_X1[ti], rb, B_NFB], [B_X1[ti]])
        T.dma("sp", yo[t0:t1, :], X1[ti][0:n, :], out_s, reads=[B_X1[ti]])

    for ti in range(9):
        for j, cg in enumerate(range(4, 8)):
            down_group(slabs3[j][0], slabs3[j][1], cg, ti)
        if ti >= 1:
            final_tile(ti - 1)
    final_tile(8)

    final_waits = [(out_s, T.dsem[out_s][1])] + [(s, T.dsem[s][1]) for s in sn_s]
    block = es.enter_context(nc.Block())
    T.emit(block, final_waits)
    es.close()
    return nc


_CACHE = {}


def _consts():
    ident = np.eye(128, dtype=np.float32)
    jj = np.arange(128)
    cmask = (jj[:, None] <= jj[None, :]).astype(np.float32)
    dq = np.zeros((128, 4, 128), np.float32)
    dk = np.zeros((128, 4, 128), np.float32)
    for h in range(4):
        dq[:, h, :] = np.exp((jj + 1) * _LG[h])[None, :]
        dk[:, h, :] = (np.exp(-(jj + 1) * _LG[h]) / 16.0)[None, :]
    return ident, cmask, dq.reshape(128, 512), dk.reshape(128, 512)


def _rope(pos):
    ang = (pos.astype(np.float32)[None, :] * _INV[:, None]).astype(np.float32)
    ang = ang.astype(np.float64)
    return np.stack([np.cos(ang), np.sin(ang)], axis=1).astype(np.float32)


def kernel(x_prompt, x_sample, state_gla, state_ret, norm_mix, w_in, w_gla_gate_up, b_gla_gate,
           gla_norm_w, w_gla_up, ret_norm_w, w_ret_up, w_out, norm_ffn, w_ffn_gate, w_ffn_up,
           w_ffn_down, norm_final):
    f = lambda a: np.ascontiguousarray(np.asarray(a, dtype=np.float32))
    x_prompt = f(x_prompt); x_sample = f(x_sample); state_gla = f(state_gla); state_ret = f(state_ret)
    if "nc" not in _CACHE:
        _CACHE["nc"] = build_program()
    nc = _CACHE["nc"]
    ident, cmask, dq, dk = _consts()
    col = lambda v: np.ascontiguousarray(f(v).reshape(-1, 128).T)
    cst = np.zeros((128, NCST), np.float32)
    cst[:, C_ID:C_ID + 128] = ident
    cst[:, C_CM:C_CM + 128] = cmask
    cst[:, C_DQ:C_DQ + 512] = dq
    cst[:, C_DK:C_DK + 512] = dk
    cst[:, C_NMIX:C_NMIX + 16] = col(norm_mix[0])
    cst[:, C_NFFN:C_NFFN + 16] = col(norm_ffn[0])
    cst[:, C_BG:C_BG + 4] = col(b_gla_gate[0])
    cst[:, C_GANW:C_GANW + 2] = col(gla_norm_w[0])
    cst[:, C_RENW:C_RENW + 2] = col(ret_norm_w[0])
    nfb = np.ascontiguousarray(np.broadcast_to(f(norm_final)[None, :], (128, D)))
    shared = {
        "w_in": f(w_in[0]), "w_gup": f(w_gla_gate_up[0]), "w_aup": f(w_gla_up[0]), "w_bup": f(w_ret_up[0]),
        "w_out": f(w_out[0]), "w_fg": f(w_ffn_gate[0]), "w_fu": f(w_ffn_up[0]), "w_fd": f(w_ffn_down[0]),
        "cst": cst, "nfb": nfb,
    }
    in_maps = []
    zeros_pre = np.zeros((T_PRE, D), np.float32)
    for c in range(8):
        b, half = c // 2, c % 2
        xo = np.concatenate([x_prompt[b, half * 1024:(half + 1) * 1024], x_sample[c * NS:(c + 1) * NS, 0]], axis=0)
        xp = x_prompt[b, 0:1024] if half == 1 else zeros_pre
        pos_o = np.concatenate([np.arange(half * 1024, (half + 1) * 1024), np.full(NS, 16384)])
        m = dict(shared)
        m.update({"xo": np.ascontiguousarray(xo), "xp": np.ascontiguousarray(xp),
                  "sg": np.ascontiguousarray(state_gla[0, c * NS:(c + 1) * NS]),
                  "sr": np.ascontiguousarray(state_ret[0, c * NS:(c + 1) * NS]),
                  "cso": _rope(pos_o), "csp": _rope(np.arange(0, 1024))})
        in_maps.append(m)
    res = run_bass_kernel_spmd(nc, in_maps, core_ids=list(range(8)))
    R = res.results
    y_prompt = np.zeros((4, 2048, D), np.float32)
    y_sample = np.zeros((128, 1, D), np.float32)
    gla_p = np.zeros((1, 4, 4, 128, 256), np.float32)
    ret_p = np.zeros((1, 4, 4, 256, 256), np.float32)
    gla_s = np.zeros((1, 128, 4, 128, 256), np.float32)
    ret_s = np.zeros((1, 128, 4, 256, 256), np.float32)
    for c in range(8):
        b, half = c // 2, c % 2
        y_prompt[b, half * 1024:(half + 1) * 1024] = R[c]["yo"][0:1024]
        y_sample[c * NS:(c + 1) * NS, 0] = R[c]["yo"][1024:1040]
        if half == 1:
            gla_p[0, b] = R[c]["gp"]
            ret_p[0, b] = R[c]["rp"]
        gla_s[0, c * NS:(c + 1) * NS] = R[c]["gs"]
        ret_s[0, c * NS:(c + 1) * NS] = R[c]["rs"]
    return (y_prompt, y_sample, gla_p, ret_p, gla_s, ret_s)
```

```python
import contextlib
import math
import numpy as np
import concourse.bass as bass
import concourse.mybir as mybir
from concourse.bass_utils import run_bass_kernel_spmd

F32 = mybir.dt.float32
BF16 = mybir.dt.bfloat16
AF = mybir.ActivationFunctionType
ALU = mybir.AluOpType

D = 2048
KT = 16
NS = 16
T_OWN = 1040
T_PRE = 1024
DFF = 5632
QA, KA, VA, GD, GA, QB, KB, VB, GB, MG = 0, 512, 1024, 2048, 2064, 3088, 4112, 5136, 6160, 7184
IN_W = 11280
EPS = 1e-6
NCST = 1344
C_ID, C_CM, C_DQ, C_DK, C_NMIX, C_NFFN, C_BG, C_GANW, C_RENW = 0, 128, 256, 768, 1280, 1296, 1312, 1316, 1318


def _ref_constants():
    lg = np.array([-1123939604, -1135839952, -1147032826, -1157619701], dtype=np.int32).view(np.float32)
    chk = np.log1p(-np.exp(np.linspace(math.log(1.0 / 32), math.log(1.0 / 512), 4)))
    assert np.allclose(lg.astype(np.float64), chk, rtol=1e-6, atol=0.0)
    inv = (10000.0 ** (-(np.arange(128, dtype=np.float64) / 128))).astype(np.float32)
    return lg.astype(np.float64), inv


_LG, _INV = _ref_constants()
GAMMA = [float(np.exp(_LG[h])) for h in range(4)]
G128 = [float(np.exp(128.0 * _LG[h])) for h in range(4)]


class Buf:
    __slots__ = ("w", "r", "excl")

    def __init__(self, excl=False):
        self.w = None
        self.r = {}
        self.excl = excl


class Trk:
    ENG = ("pe", "act", "dve", "pool", "sp")

    def __init__(self, nc, es):
        self.nc = nc
        self.es = es
        self.sem = {}
        self.cnt = {}
        self.prog = {}
        self.waited = {}
        for e in self.ENG:
            self.sem[e] = es.enter_context(nc.semaphore("sem_" + e))
            self.cnt[e] = 0
            self.prog[e] = []
            self.waited[e] = {}
        self.dsem = {}

    def new_dsem(self, name):
        self.dsem[name] = [self.es.enter_context(self.nc.semaphore(name)), 0]
        return name

    def _h(self, key):
        return self.sem[key] if key in self.sem else self.dsem[key][0]

    def _need(self, eng, ev, waits):
        if ev is None:
            return
        k, v = ev
        if k == eng and eng == "pe":
            return
        if self.waited[eng].get(k, 0) >= v:
            return
        self.waited[eng][k] = v
        waits.append((k, v))

    def _deps(self, eng, reads, writes):
        waits = []
        for b in reads:
            self._need(eng, b.w, waits)
            if b.excl:
                for k, v in b.r.items():
                    self._need(eng, (k, v), waits)
        for b in writes:
            self._need(eng, b.w, waits)
            for k, v in b.r.items():
                self._need(eng, (k, v), waits)
        return waits

    def op(self, eng, fn, reads=(), writes=(), inc=True):
        waits = self._deps(eng, reads, writes)
        if inc:
            self.cnt[eng] += 1
            val = self.cnt[eng]
        else:
            val = self.cnt[eng] + 1
        for b in reads:
            b.r[eng] = max(b.r.get(eng, 0), val)
        for b in writes:
            b.w = (eng, val)
            b.r = {}
        self.prog[eng].append((waits, fn, inc))

    def dma(self, q, out, in_, dsem, reads=(), writes=()):
        waits = self._deps(q, reads, writes)
        d = self.dsem[dsem]
        d[1] += 16
        val = d[1]
        for b in reads:
            b.r[dsem] = max(b.r.get(dsem, 0), val)
        for b in writes:
            b.w = (dsem, val)
            b.r = {}
        h = d[0]
        self.prog[q].append((waits, lambda e, out=out, in_=in_, h=h: e.dma_start(out=out, in_=in_).then_inc(h, 16), None))

    def barrier(self, engs=("pe", "act", "dve", "sp")):
        for e in engs:
            waits = []
            for e2 in engs:
                if e2 != e and self.cnt[e2] > 0:
                    self._need(e, (e2, self.cnt[e2]), waits)
            for name, (h, c) in self.dsem.items():
                if c > 0 and not name.startswith("ring"):
                    self._need(e, (name, c), waits)
            if waits:
                self.prog[e].append((waits, None, False))

    def emit(self, block, final_waits):
        names = {"pe": "tensor", "act": "scalar", "dve": "vector", "pool": "gpsimd", "sp": "sync"}
        for e in self.ENG:
            def body(eng, e=e):
                for waits, fn, inc in self.prog[e]:
                    for k, v in waits:
                        eng.wait_ge(self._h(k), v)
                    if fn is None:
                        continue
                    ins = fn(eng)
                    if inc:
                        ins.then_inc(self.sem[e], 1)
                if e == "sp":
                    for k, v in final_waits:
                        eng.wait_ge(self._h(k), v)
            getattr(block, names[e])(body)


def build_program():
    nc = bass.Bass("TRN2", target_bir_lowering=False)

    def din(n, s):
        return nc.dram_tensor(n, s, F32, kind="ExternalInput").ap()

    def dout(n, s):
        return nc.dram_tensor(n, s, F32, kind="ExternalOutput").ap()

    xo = din("xo", [T_OWN, D]); xp = din("xp", [T_PRE, D])
    sg = din("sg", [NS, 4, 128, 256]); sr = din("sr", [NS, 4, 256, 256])
    w_in = din("w_in", [D, IN_W]); w_gup = din("w_gup", [16, 512])
    w_aup = din("w_aup", [1024, D]); w_bup = din("w_bup", [1024, D]); w_out = din("w_out", [D, D])
    w_fg = din("w_fg", [D, DFF]); w_fu = din("w_fu", [D, DFF]); w_fd = din("w_fd", [DFF, D])
    cst = din("cst", [128, NCST]); nfb = din("nfb", [128, D])
    cso = din("cso", [128, 2, T_OWN]); csp = din("csp", [128, 2, T_PRE])
    yo = dout("yo", [T_OWN, D]); gp = dout("gp", [4, 128, 256]); rp = dout("rp", [4, 256, 256])
    gs = dout("gs", [NS, 4, 128, 256]); rs = dout("rs", [NS, 4, 256, 256])

    es = contextlib.ExitStack()
    NA = 53008
    A = es.enter_context(nc.sbuf_tensor("arena", [128, NA], F32))
    AB = A.bitcast(BF16)
    ps = [es.enter_context(nc.psum_tensor("ps%d" % i, [128, 512], F32)) for i in range(8)]
    PB = [Buf(excl=True) for _ in range(8)]
    T = Trk(nc, es)

    def Fv(off, n):
        return A[:, off // 4: off // 4 + n]

    def Hv(off, n):
        return AB[:, off // 2: off // 2 + n]

    RING = [i * 8192 for i in range(4)]
    O_CST = 32768
    O_IDB = 38144
    O_WGU = 38400
    O_SML = 39424
    O_ONES = 40448
    O_GDT = 40960
    O_HT = 43072
    R0 = 76352
    O_SST = R0
    W = R0 + 12288
    O_XT = [0, 0]; O_XN = [0, 0]; O_JNK = 0
    O_QT = [W, W + 4160]; O_KT = [W + 8320, W + 12480]; O_VS = [W + 16640, W + 21248]
    O_S2 = W + 25856
    O_SCR = [W + 35072 + i * 4160 for i in range(4)]
    O_S2B = W + 51712
    O_SBF = W + 60928
    O_S0 = [W + 61952 + i * 1024 for i in range(4)]
    O_SN = [W + 66048 + i * 1024 for i in range(4)]
    O_SNB = [W + 70144 + i * 512 for i in range(4)]
    O_VM = [W + 72192 + i * 512 for i in range(2)]
    O_KST = W + 73216
    O_OSTS = W + 73728
    O_ATM = [W + 73984 + i * 256 for i in range(2)]
    O_KTK = [W + 74496 + i * 512 for i in range(2)]
    O_U = [W + 75520 + i * 2048 for i in range(2)]
    O_ON = [W + 79616 + i * 512 for i in range(2)] + [W + 88320]
    O_OSB = [W + 80640 + i * 1024 for i in range(2)] + [W + 88832]
    O_JNK2 = W + 82688
    O_TH = [W + 83712 + i * 1024 for i in range(2)]
    O_OT = W + 89856
    O_MT = W
    O_MG = W + 33280
    O_X1 = W + 33280
    O_LATE = W + 107008
    O_XT = [O_OT, O_OT + 8192, O_OT + 16384]; O_XN = [O_OT + 24576, O_OT + 28672]; O_JNK = 0
    assert O_OT + 33280 <= NA * 4 and O_LATE + 12288 <= NA * 4

    ring_modes = {}
    for n_ in (4, 8):
        sz_ = 32768 // n_
        ring_modes[n_] = ([Hv(i * sz_, sz_ // 2) for i in range(n_)], [Buf() for _ in range(n_)],
                          [T.new_dsem("ring%d_%d" % (n_, i)) for i in range(n_)])
    cur_mode = [4]
    slab_ctr = [0]

    def switch_mode(n_new):
        n_old = cur_mode[0]
        old = ring_modes[n_old][1]
        new = ring_modes[n_new][1]
        for j, nb in enumerate(new):
            lo, hi = j * 32768 // n_new, (j + 1) * 32768 // n_new
            nb.w = None
            nb.r = {}
            for i, ob in enumerate(old):
                olo, ohi = i * 32768 // n_old, (i + 1) * 32768 // n_old
                if olo < hi and lo < ohi:
                    evs = list(ob.r.items()) + ([ob.w] if ob.w is not None else [])
                    for (k, v) in evs:
                        nb.r[k] = max(nb.r.get(k, 0), v)
        cur_mode[0] = n_new
        slab_ctr[0] = 0
    CST = Fv(O_CST, NCST); B_CST = Buf()
    IDF = CST[:, C_ID:C_ID + 128]
    CMASK = CST[:, C_CM:C_CM + 128]
    IDB = Hv(O_IDB, 128); B_IDB = Buf()
    WGU = Hv(O_WGU, 512); B_WGU = Buf()
    SML = Fv(O_SML, 256); B_SML = Buf()
    ELAST = [SML[:, 96 + 8 * i: 104 + 8 * i] for i in range(3)]; B_ELAST = [Buf(), Buf(), Buf()]
    EGS = [SML[:, 128 + 16 * i: 144 + 16 * i] for i in range(3)]; B_EGS = [Buf(), Buf(), Buf()]
    ONES = Fv(O_ONES, 128); B_ONES = Buf()
    GDT = Hv(O_GDT, T_OWN); B_GDT = Buf()
    HT = Hv(O_HT, 16 * T_OWN).rearrange("p (k t) -> p k t", k=16); B_HT = Buf()
    OT = Hv(O_OT, 16 * T_OWN).rearrange("p (k t) -> p k t", k=16); B_OT = Buf()
    MT = Hv(O_MT, 16 * T_OWN).rearrange("p (k t) -> p k t", k=16); B_MT = Buf()
    SST = Fv(O_SST, 12 * 256); B_SST = [Buf() for _ in range(8)]
    NEGB = SML[:, 0:4]; EPSC = SML[:, 4:5]; LNQ = SML[:, 5:6]
    sml_rot = [0]
    sml_bufs = [Buf() for _ in range(10)]

    def sml():
        i = sml_rot[0] % 10
        sml_rot[0] += 1
        return SML[:, 16 + i * 8: 16 + i * 8 + 8], sml_bufs[i]

    misc_s = T.new_dsem("misc")
    misc2_s = T.new_dsem("misc2")
    misc3_s = T.new_dsem("misc3")
    out_s = T.new_dsem("outs")
    xs = [T.new_dsem("xs0"), T.new_dsem("xs1"), T.new_dsem("xs2")]
    cs_s = [T.new_dsem("cs0"), T.new_dsem("cs1")]
    s0_s = [T.new_dsem("s0_%d" % i) for i in range(4)]
    sn_s = [T.new_dsem("sn_%d" % i) for i in range(4)]
    x1_s = [T.new_dsem("x1_%d" % i) for i in range(9)]

    gemm_banks = [0, 1, 2]
    bank_ctr = [0]

    def gb():
        b = gemm_banks[bank_ctr[0] % len(gemm_banks)]
        bank_ctr[0] += 1
        return b

    OWN_BLOCKS = [(0, 512), (512, 1024), (1024, 1040)]
    PRE_BLOCKS = [(0, 512), (512, 1024)]
    OWN_TILES = [(i * 128, (i + 1) * 128) for i in range(8)] + [(1024, 1040)]
    PRE_TILES = [(i * 128, (i + 1) * 128) for i in range(8)]

    def load_slab(pieces, kt):
        ring_v, ring_b, ring_s = ring_modes[cur_mode[0]]
        i = slab_ctr[0] % (cur_mode[0] if cur_mode[0] == 4 else 6)
        slab_ctr[0] += 1
        ncols = sum(p[1] for p in pieces)
        view = ring_v[i][:, 0:kt * ncols].rearrange("p (k n) -> p k n", k=kt)
        for (src, n, c0) in pieces:
            T.dma("pool", view[:, :, c0:c0 + n], src.rearrange("(k p) n -> p k n", p=128), ring_s[i], writes=[ring_b[i]])
        return view, ring_b[i]

    def mm(out, lhsT, rhs, start, stop, reads, bank, inc):
        T.op("pe", lambda e, out=out, lhsT=lhsT, rhs=rhs, start=start, stop=stop: e.matmul(out, lhsT=lhsT, rhs=rhs, start=start, stop=stop),
             reads=reads, writes=[PB[bank]], inc=inc)

    def act(out, in_, func, reads, writes, bias=None, scale=None, accum=None):
        kw = {}
        if bias is not None:
            kw["bias"] = bias
        if scale is not None:
            kw["scale"] = scale
        if accum is not None:
            kw["accum_out"] = accum
        T.op("act", lambda e, out=out, in_=in_, func=func, kw=kw: e.activation(out=out, in_=in_, func=func, **kw), reads=reads, writes=writes)

    def tt(out, in0, in1, op, reads, writes):
        T.op("dve", lambda e, out=out, in0=in0, in1=in1, op=op: e.tensor_tensor(out=out, in0=in0, in1=in1, op=op), reads=reads, writes=writes)

    def stt(out, in0, scalar, in1, op0, op1, reads, writes):
        T.op("dve", lambda e, out=out, in0=in0, scalar=scalar, in1=in1, op0=op0, op1=op1: e.scalar_tensor_tensor(out=out, in0=in0, scalar=scalar, in1=in1, op0=op0, op1=op1),
             reads=reads, writes=writes)

    def tsmul(out, in0, scalar, reads, writes):
        T.op("dve", lambda e, out=out, in0=in0, scalar=scalar: e.tensor_scalar(out=out, in0=in0, scalar1=scalar, scalar2=0.0, op0=ALU.mult, op1=ALU.add),
             reads=reads, writes=writes)

    def memset(ap, val, writes):
        T.op("dve", lambda e, ap=ap, val=val: e.memset(ap, val), writes=writes)

    def rstd_from_ss(ss_ap, n_feat, sbuf, n):
        sc, b = sml()
        act(sc[0:n, 0:1], ss_ap, AF.Ln, reads=[sbuf, B_SML], writes=[b], scale=1.0 / n_feat, bias=EPSC[0:n, :])
        act(sc[0:n, 1:2], sc[0:n, 0:1], AF.Exp, reads=[b], writes=[b], scale=-0.5)
        return sc[0:n, 1:2], b

    T.dma("sp", CST, cst, misc_s, writes=[B_CST])
    T.dma("pool", WGU[0:16, :], w_gup, misc2_s, writes=[B_WGU])
    T.op("dve", lambda e: e.tensor_copy(out=IDB, in_=IDF), reads=[B_CST], writes=[B_IDB])
    memset(ONES, 1.0, [B_ONES])
    tsmul(NEGB, CST[:, C_BG:C_BG + 4], -1.0, [B_CST], [B_SML])
    memset(EPSC, EPS, [B_SML])
    memset(LNQ, math.log(128.0 ** -0.5), [B_SML])

    p0_bufs = []
    P0B = [Buf() for _ in range(5)]

    def norm_transpose(src_dram, tiles, nwc, from_x1=None, only_setup=False, ticker=None):
        XT = [Fv(O_XT[i], D) for i in range(3)]
        XN = [Hv(O_XN[i] if from_x1 is None else O_LATE + 8192 + i * 4096, D) for i in range(2)]
        if from_x1 is None:
            b_xt, b_xn = P0B[0:3], P0B[3:5]
            p0_bufs[:] = P0B
        else:
            b_xt = [Buf(), Buf(), Buf()]; b_xn = B_XN2

        def do_tile(ti):
            t0, t1 = tiles[ti]
            n = t1 - t0
            k = ti % 2
            if from_x1 is None:
                k3 = ti % 3
                T.dma("sp", XT[k3][0:n, :], src_dram[t0:t1, :], xs[k3], writes=[b_xt[k3]])
                xin, bx = XT[k3], b_xt[k3]
            else:
                xin, bx = from_x1[ti]
            sc, b = sml()
            act(XN[k][0:n, :], xin[0:n, :], AF.Square, reads=[bx], writes=[b_xn[k], b], accum=sc[0:n, 2:3])
            r, rb = rstd_from_ss(sc[0:n, 2:3], D, b, n)
            tsmul(XN[k][0:n, :], xin[0:n, :], r, [bx, rb], [b_xn[k]])
            for g in range(4):
                bk = gb()
                for j in range(4):
                    kt = g * 4 + j
                    mm(ps[bk][:, j * 128: j * 128 + n], XN[k][0:n, kt * 128:(kt + 1) * 128], IDB[0:n, 0:n], True, True,
                       [b_xn[k], B_IDB], bk, j == 3)
                tt(HT[:, g * 4:(g + 1) * 4, t0:t1], ps[bk][:, :].rearrange("p (a b) -> p a b", a=4)[:, :, 0:n],
                   CST[:, nwc + g * 4: nwc + (g + 1) * 4].unsqueeze(2).broadcast_to([128, 4, n]), ALU.mult,
                   [PB[bk], B_CST], [B_HT])
                if ticker is not None:
                    ticker()
        if only_setup:
            return do_tile
        for ti in range(len(tiles)):
            do_tile(ti)

    B_XN2 = [Buf(), Buf()]

    QT = [Hv(O_QT[i], 2 * T_OWN).rearrange("p (d t) -> p d t", d=2) for i in range(2)]
    KTt = [Hv(O_KT[i], 2 * T_OWN).rearrange("p (d t) -> p d t", d=2) for i in range(2)]
    VS = [Hv(O_VS[i], 9 * 256).rearrange("p (c e) -> p c e", c=9) for i in range(2)]
    S2 = [Fv(o, 9 * 256).rearrange("p (c e) -> p c e", c=9) for o in (O_S2, O_S2B)]
    SCR = [Fv(O_SCR[i], T_OWN) for i in range(4)]
    B_QT = [Buf(), Buf()]; B_KT = [Buf(), Buf()]; B_VS = [Buf(), Buf()]; B_S2 = [Buf(), Buf()]
    B_SCR = [Buf() for _ in range(4)]
    SBF = Hv(O_SBF, 512); B_SBF = Buf()
    S0 = [Fv(O_S0[i], 256) for i in range(4)]; B_S0 = [Buf() for _ in range(4)]
    SN = [Fv(O_SN[i], 256) for i in range(4)]; B_SN = [Buf() for _ in range(4)]
    SNB = [Hv(O_SNB[i], 256) for i in range(4)]; B_SNB = [Buf() for _ in range(4)]
    VM = [Hv(O_VM[i], 256) for i in range(2)]; B_VM = [Buf(), Buf()]
    KST = Hv(O_KST, 256); B_KST = Buf()
    OSTS = Fv(O_OSTS, 32); B_OSTS = Buf()
    ATM = [Hv(O_ATM[i], 128) for i in range(2)]; B_ATM = [Buf(), Buf()]
    KTK = [Hv(O_KTK[i], 256) for i in range(2)]; B_KTK = [Buf(), Buf()]
    U = [Fv(O_U[i], 512) for i in range(2)]; B_U = [Buf(), Buf()]
    ON = [Hv(O_ON[i], 256) for i in range(3)]; B_ON = [Buf(), Buf(), Buf()]
    OSB = [Fv(O_OSB[i], 256) for i in range(3)]; B_OSB = [Buf(), Buf(), Buf()]
    JNK2 = Hv(O_JNK2, 256); B_JNK2 = Buf()
    TH = [Fv(O_TH[i], 256) for i in range(2)]; B_TH = [Buf(), Buf()]
    rot = {"atm": 0, "ktk": 0, "u": 0, "on": 0, "osb": 0, "th": 0, "vm": 0, "s0": 0, "snb": 0}

    def nxt(k, m=2):
        i = rot[k] % m
        rot[k] += 1
        return i

    def sst_view(br, h):
        nd = 1 if br == 0 else 2
        off = h * 256 if br == 0 else 1024 + h * 512
        return SST[:, off: off + nd * 256], B_SST[br * 4 + h]

    def post_a(br, o_ps, o_bank, n, ti, s2b, dedicated=False):
        i = 2 if dedicated else nxt("on")
        if br == 0:
            sc, b = sml()
            act(JNK2[0:n, :], o_ps, AF.Square, reads=[PB[o_bank]], writes=[B_JNK2, b], accum=sc[0:n, 2:3])
            r, rb = rstd_from_ss(sc[0:n, 2:3], 256, b, n)
            stt(ON[i][0:n, :], o_ps, r, S2[s2b][0:n, ti, :], ALU.mult, ALU.mult, [PB[o_bank], rb, B_S2[s2b]], [B_ON[i]])
        else:
            j = 2 if dedicated else nxt("osb")
            sc, b = sml()
            act(OSB[j][0:n, :], o_ps, AF.Identity, reads=[PB[o_bank]], writes=[B_OSB[j], b], accum=sc[0:n, 2:3])
            tsmul(sc[0:n, 3:4], sc[0:n, 2:3], -1.0 / 256, [b], [b])
            act(JNK2[0:n, :], OSB[j][0:n, :], AF.Square, reads=[B_OSB[j], b], writes=[B_JNK2, b], bias=sc[0:n, 3:4], accum=sc[0:n, 4:5])
            r, rb = rstd_from_ss(sc[0:n, 4:5], 256, b, n)
            stt(OSB[j][0:n, :], OSB[j][0:n, :], sc[0:n, 3:4], S2[s2b][0:n, ti, :], ALU.add, ALU.mult, [B_OSB[j], b, B_S2[s2b]], [B_OSB[j]])
            tsmul(ON[i][0:n, :], OSB[j][0:n, :], r, [B_OSB[j], rb], [B_ON[i]])
        return i

    def post_b(br, h, i, n, t0, t1):
        nwc = C_GANW if br == 0 else C_RENW
        for ec in range(2):
            mm(ps[5][:, ec * 128: ec * 128 + n], ON[i][0:n, ec * 128:(ec + 1) * 128], IDB[0:n, 0:n], True, True,
               [B_ON[i], B_IDB], 5, ec == 1)
        k0 = br * 8 + h * 2
        tt(OT[:, k0:k0 + 2, t0:t1], ps[5][:, 0:256].rearrange("p (a b) -> p a b", a=2)[:, :, 0:n],
           CST[:, nwc:nwc + 2].unsqueeze(2).broadcast_to([128, 2, n]), ALU.mult, [PB[5], B_CST], [B_OT] + p0_bufs)
        del p0_bufs[:]

    def attention(br, h, hb, prefix, elast_ap, elast_buf):
        nd = 1 if br == 0 else 2
        S, bS = sst_view(br, h)
        if prefix:
            memset(S, 0.0, [bS])
        else:
            act(SBF[:, 0:nd * 256], S, AF.Copy, reads=[bS], writes=[B_SBF])
        pend_b = None
        for c in range(8):
            ch0, ch1 = c * 128, (c + 1) * 128
            if not prefix:
                for dt in range(nd):
                    mm(ps[3][:, 0:128], KTt[hb][:, dt, ch0:ch1], QT[hb][:, dt, ch0:ch1], dt == 0, dt == nd - 1,
                       [B_KT[hb], B_QT[hb]], 3, dt == nd - 1)
                ia = nxt("atm")
                tt(ATM[ia], ps[3][:, 0:128], CMASK, ALU.mult, [PB[3], B_CST], [B_ATM[ia]])
                if pend_b is not None:
                    post_b(*pend_b)
                    pend_b = None
                yield
                mm(ps[4][:, 0:256], ATM[ia], VS[hb][:, c, :], True, False, [B_ATM[ia], B_VS[hb]], 4, False)
                for dt in range(nd):
                    mm(ps[4][:, 0:256], QT[hb][:, dt, ch0:ch1], SBF[:, dt * 256:(dt + 1) * 256], False, dt == nd - 1,
                       [B_QT[hb], B_SBF], 4, dt == nd - 1)
            for dt in range(nd):
                mm(ps[3][:, 128 + dt * 128: 256 + dt * 128], KTt[hb][:, dt, ch0:ch1], IDB, True, True, [B_KT[hb], B_IDB], 3, dt == nd - 1)
            ik = nxt("ktk")
            if br == 0:
                act(KTK[ik][:, 0:nd * 128], ps[3][:, 128:128 + nd * 128], AF.Copy, reads=[PB[3]], writes=[B_KTK[ik]])
            else:
                act(KTK[ik][:, 0:nd * 128], ps[3][:, 128:128 + nd * 128], AF.Copy, reads=[PB[3]], writes=[B_KTK[ik]], scale=G128[h])
            if not prefix:
                io = post_a(br, ps[4][:, 0:256], 4, 128, c, hb)
                pend_b = (br, h, io, 128, ch0, ch1)
            yield
            for dt in range(nd):
                mm(ps[6][:, dt * 256:(dt + 1) * 256], KTK[ik][:, dt * 128:(dt + 1) * 128], VS[hb][:, c, :], True, True,
                   [B_KTK[ik], B_VS[hb]], 6, dt == nd - 1)
            if br == 0:
                j = nxt("u")
                el = elast_ap[:, c:c + 1]
                rd = [elast_buf]
                act(U[j][:, 0:nd * 256], ps[6][:, 0:nd * 256], AF.Copy, reads=[PB[6]] + rd, writes=[B_U[j]], scale=el)
                stt(S, S, el, U[j][:, 0:nd * 256], ALU.mult, ALU.add, [bS, B_U[j]] + rd, [bS])
            else:
                stt(S, S, G128[h], ps[6][:, 0:nd * 256], ALU.mult, ALU.add, [bS, PB[6]], [bS])
            if not prefix:
                act(SBF[:, 0:nd * 256], S, AF.Copy, reads=[bS], writes=[B_SBF])
            yield
        if pend_b is not None:
            post_b(*pend_b)
            yield
        if not prefix:
            if br == 0:
                T.dma("sp", gp[h], S, out_s, reads=[bS])
            else:
                T.dma("sp", rp[h].rearrange("(d p) e -> p d e", p=128), S.rearrange("p (d e) -> p d e", d=2), out_s, reads=[bS])

    def sample_step(br, h, hb, egs_ap, egs_buf):
        nd = 1 if br == 0 else 2
        for dt in range(nd):
            mm(ps[7][0:16, dt * 128:(dt + 1) * 128], KTt[hb][:, dt, 1024:1040], IDB, True, True, [B_KT[hb], B_IDB], 7, dt == nd - 1)
        act(KST[0:16, 0:nd * 128], ps[7][0:16, 0:nd * 128], AF.Copy, reads=[PB[7]], writes=[B_KST])
        yield

        def os_mm(t, snb_idx):
            for ec in range(2):
                for dt in range(nd):
                    k = snb_idx[dt]
                    mm(ps[5][:, 256 + ec * 16 + t: 256 + ec * 16 + t + 1], SNB[k][:, ec * 128:(ec + 1) * 128],
                       QT[hb][:, dt, 1024 + t: 1025 + t], dt == 0, dt == nd - 1, [B_SNB[k], B_QT[hb]], 5, dt == nd - 1)
        def vmask(iv_, t_):
            if br == 1:
                act(VM[iv_][0:16, :], VS[hb][0:16, 8, :], AF.Copy, reads=[B_VS[hb], B_CST], writes=[B_VM[iv_]], scale=IDF[0:16, t_:t_ + 1])
            else:
                tsmul(VM[iv_][0:16, :], VS[hb][0:16, 8, :], IDF[0:16, t_:t_ + 1], [B_VS[hb], B_CST], [B_VM[iv_]])

        prev = None
        iv_next = nxt("vm")
        vmask(iv_next, 0)
        yield
        for t in range(NS):
            if prev is not None:
                os_mm(*prev)
            iv = iv_next
            js = []
            for dt in range(nd):
                j = nxt("s0", 4)
                src = sg[t, h] if br == 0 else sr[t, h, dt * 128:(dt + 1) * 128, :]
                T.dma("sp", S0[j], src, s0_s[j], writes=[B_S0[j]])
                mm(ps[7][:, dt * 256:(dt + 1) * 256], KST[0:16, dt * 128:(dt + 1) * 128], VM[iv][0:16, :], True, True, [B_KST, B_VM[iv]], 7, dt == nd - 1)
                js.append(j)
            if t + 1 < NS:
                iv_next = nxt("vm")
                vmask(iv_next, t + 1)
            snb_idx = []
            for dt in range(nd):
                j = js[dt]
                dst = gs[t, h] if br == 0 else rs[t, h, dt * 128:(dt + 1) * 128, :]
                if br == 0:
                    el = egs_ap[:, t:t + 1]; rd = [egs_buf]
                else:
                    el = GAMMA[h]; rd = []
                stt(SN[j], S0[j], el, ps[7][:, dt * 256:(dt + 1) * 256], ALU.mult, ALU.add, [B_S0[j], PB[7]] + rd, [B_SN[j]])
                k = nxt("snb", 4)
                act(SNB[k], SN[j], AF.Copy, reads=[B_SN[j]], writes=[B_SNB[k]])
                T.dma("act", dst, SN[j], sn_s[j], reads=[B_SN[j]])
                snb_idx.append(k)
            prev = (t, snb_idx)
            yield
        os_mm(*prev)
        act(OSTS, ps[5][:, 256:288], AF.Copy, reads=[PB[5]], writes=[B_OSTS])
        yield
        for ec in range(2):
            mm(ps[4][0:16, ec * 128:(ec + 1) * 128], OSTS[:, ec * 16:(ec + 1) * 16], IDF, True, True, [B_OSTS, B_CST], 4, ec == 1)
        io = post_a(br, ps[4][0:16, 0:256], 4, 16, 8, hb, dedicated=True)
        yield
        post_b(br, h, io, 16, 1024, 1040)
        yield

    pending = []

    rr = [0]

    def tick():
        while pending:
            i = rr[0] % len(pending)
            rr[0] += 1
            try:
                next(pending[i])
                return
            except StopIteration:
                pending.pop(i)

    def drain():
        while pending:
            tick()

    def proj_fm(slab, sbuf, c0, blocks, evac):
        for (t0, t1) in blocks:
            n = t1 - t0
            bk = gb()
            for kt in range(KT):
                mm(ps[bk][:, 0:n], slab[:, kt, c0:c0 + 128], HT[:, kt, t0:t1], kt == 0, kt == KT - 1, [sbuf, B_HT], bk, kt == KT - 1)
                if kt == 7 and n > 16:
                    tick()
            evac(ps[bk][:, 0:n], bk, t0, t1)
            tick()

    def proj_tm(slab, sbuf, tiles, evac):
        for ti, (t0, t1) in enumerate(tiles):
            n = t1 - t0
            bk = gb()
            for kt in range(KT):
                mm(ps[bk][0:n, 0:256], HT[:, kt, t0:t1], slab[:, kt, 0:256], kt == 0, kt == KT - 1, [B_HT, sbuf], bk, kt == KT - 1)
                if kt == 7 and n > 16:
                    tick()
            evac(ps[bk][0:n, 0:256], bk, ti, n)
            tick()

    def silu_evac(dst_fn, dbuf):
        def ev(p, bk, ti, n):
            i = nxt("th")
            act(TH[i][0:n, :], p, AF.Exp, reads=[PB[bk]], writes=[B_TH[i]], scale=-1.0)
            act(TH[i][0:n, :], TH[i][0:n, :], AF.Ln, reads=[B_TH[i]], writes=[B_TH[i]], bias=1.0)
            act(TH[i][0:n, :], TH[i][0:n, :], AF.Exp, reads=[B_TH[i]], writes=[B_TH[i]], scale=-1.0)
            tt(dst_fn(ti, n), p, TH[i][0:n, :], ALU.mult, [PB[bk], B_TH[i]], [dbuf])
        return ev

    def win_pass(prefix, final_drain=True):
        blocks = PRE_BLOCKS if prefix else OWN_BLOCKS
        tiles = PRE_TILES if prefix else OWN_TILES
        csd = csp if prefix else cso
        ntk = blocks[-1][1]
        slab, sb = load_slab([(w_in[:, GD:GD + 16], 16, 0)], KT)
        for (t0, t1) in blocks:
            n = t1 - t0
            bk = gb()
            for kt in range(KT):
                mm(ps[bk][0:16, 0:n], slab[:, kt, 0:16], HT[:, kt, t0:t1], kt == 0, kt == KT - 1, [sb, B_HT], bk, kt == KT - 1)
            act(GDT[0:16, t0:t1], ps[bk][0:16, 0:n], AF.Copy, reads=[PB[bk]], writes=[B_GDT])

        def e_stage(h):
            hb = h % 3
            for (t0, t1) in blocks:
                n = t1 - t0
                bk = gb()
                mm(ps[bk][:, 0:n], WGU[0:16, h * 128:(h + 1) * 128], GDT[0:16, t0:t1], True, True, [B_WGU, B_GDT], bk, True)
                act(SCR[0][:, t0:t1], ps[bk][:, 0:n], AF.Exp, reads=[PB[bk], B_SML], writes=[B_SCR[0]], scale=-1.0, bias=NEGB[:, h:h + 1])
                tick()
            act(SCR[0][:, 0:ntk], SCR[0][:, 0:ntk], AF.Ln, reads=[B_SCR[0]], writes=[B_SCR[0]], bias=1.0)
            for c in range(8):
                T.op("dve", lambda e, c=c: e.tensor_tensor_scan(out=SCR[1][:, c * 128:(c + 1) * 128], data0=ONES[:, 0:128],
                                                                  data1=SCR[0][:, c * 128:(c + 1) * 128], initial=0.0, op0=ALU.mult, op1=ALU.add),
                     reads=[B_SCR[0], B_ONES], writes=[B_SCR[1]])
            if not prefix:
                act(SCR[2][:, 0:1024], SCR[1][:, 0:1024], AF.Exp, reads=[B_SCR[1], B_SML], writes=[B_SCR[2]], scale=-1.0 / 16, bias=LNQ)
                memset(SCR[2][:, 1024:1040], 128.0 ** -0.5, [B_SCR[2]])
            act(SCR[3][:, 0:1024], SCR[1][:, 0:1024], AF.Exp, reads=[B_SCR[1]], writes=[B_SCR[3]], scale=1.0 / 16)
            if not prefix:
                memset(SCR[3][:, 1024:1040], 1.0, [B_SCR[3]])
            act(ELAST[hb], SCR[1][:, 127:1024:128], AF.Exp, reads=[B_SCR[1]], writes=[B_ELAST[hb]], scale=-1.0 / 16)
            if not prefix:
                act(EGS[hb], SCR[0][:, 1024:1040], AF.Exp, reads=[B_SCR[0]], writes=[B_EGS[hb]], scale=-1.0 / 16)

        hcount = 0
        for br in (1, 0):
            for h in range(4):
                hb = hcount % 2
                hcount += 1
                elast = egs = None
                ebuf = None
                if br == 0:
                    elast, ebuf = ELAST[h % 3], B_ELAST[h % 3]
                    egs = EGS[h % 3]
                    if prefix:
                        slab, sb = load_slab([(w_in[:, KA + h * 128: KA + (h + 1) * 128], 128, 0)], KT)
                        kc0 = 0
                    else:
                        slab, sb = load_slab([(w_in[:, QA + h * 128: QA + (h + 1) * 128], 128, 0),
                                              (w_in[:, KA + h * 128: KA + (h + 1) * 128], 128, 128)], KT)
                        kc0 = 128
                    if not prefix:
                        proj_fm(slab, sb, 0, blocks, lambda p, bk, t0, t1: tt(QT[hb][:, 0, t0:t1], p, SCR[2][:, t0:t1], ALU.mult,
                                                                             [PB[bk], B_SCR[2]], [B_QT[hb]]))
                    proj_fm(slab, sb, kc0, blocks, lambda p, bk, t0, t1: tt(KTt[hb][:, 0, t0:t1], p, SCR[3][:, t0:t1], ALU.mult,
                                                                           [PB[bk], B_SCR[3]], [B_KT[hb]]))
                    if h < 3:
                        e_stage(h + 1)
                    vcol, gcol = VA + h * 256, GA + h * 256
                else:
                    def rot_block(slab, sb, t0, t1, dec_col, scale_s, dst, dbuf):
                        n = t1 - t0
                        cs = SCR[3]
                        T.dma("sp", cs[:, 0:1040].rearrange("p (a t) -> p a t", a=2)[:, :, 0:n], csd[:, :, t0:t1], cs_s[0], writes=[B_SCR[3]])
                        cosb, sinb = cs[:, 0:n], cs[:, 520:520 + n]
                        ctab, stab = SCR[0][:, 0:n], SCR[0][:, 520:520 + n]
                        if n == 512:
                            dec = CST[:, dec_col + h * 128: dec_col + (h + 1) * 128].unsqueeze(1).broadcast_to([128, 4, 128])
                            tt(ctab.rearrange("p (a b) -> p a b", a=4), cosb.rearrange("p (a b) -> p a b", a=4), dec, ALU.mult,
                               [B_SCR[3], B_CST], [B_SCR[0]])
                            tt(stab.rearrange("p (a b) -> p a b", a=4), sinb.rearrange("p (a b) -> p a b", a=4), dec, ALU.mult,
                               [B_SCR[3], B_CST], [B_SCR[0]])
                        else:
                            tsmul(ctab, cosb, scale_s, [B_SCR[3]], [B_SCR[0]])
                            tsmul(stab, sinb, scale_s, [B_SCR[3]], [B_SCR[0]])
                        b1, b2 = gb(), gb()
                        for dt, bk in ((0, b1), (1, b2)):
                            for kt in range(KT):
                                mm(ps[bk][:, 0:n], slab[:, kt, dt * 128:(dt + 1) * 128], HT[:, kt, t0:t1], kt == 0, kt == KT - 1,
                                   [sb, B_HT], bk, kt == KT - 1)
                                if kt == 7 and n > 16:
                                    tick()
                        x1, x2 = ps[b1][:, 0:n], ps[b2][:, 0:n]
                        ta, tb = SCR[1][:, 0:n], SCR[1][:, 520:520 + n]
                        tc, td = SCR[2][:, 0:n], SCR[2][:, 520:520 + n]
                        tt(ta, x1, ctab, ALU.mult, [PB[b1], B_SCR[0]], [B_SCR[1]])
                        tt(tb, x2, stab, ALU.mult, [PB[b2], B_SCR[0]], [B_SCR[1]])
                        tt(tc, x1, stab, ALU.mult, [PB[b1], B_SCR[0]], [B_SCR[2]])
                        tt(td, x2, ctab, ALU.mult, [PB[b2], B_SCR[0]], [B_SCR[2]])
                        tt(dst[:, 0, t0:t1], ta, tb, ALU.subtract, [B_SCR[1]], [dbuf])
                        tt(dst[:, 1, t0:t1], tc, td, ALU.add, [B_SCR[2]], [dbuf])
                        tick()
                    if not prefix:
                        slab, sb = load_slab([(w_in[:, QB + h * 256: QB + (h + 1) * 256], 256, 0)], KT)
                        for (t0, t1) in blocks:
                            rot_block(slab, sb, t0, t1, C_DQ, 1.0, QT[hb], B_QT[hb])
                    slab, sb = load_slab([(w_in[:, KB + h * 256: KB + (h + 1) * 256], 256, 0)], KT)
                    for (t0, t1) in blocks:
                        rot_block(slab, sb, t0, t1, C_DK, 1.0 / 16, KTt[hb], B_KT[hb])
                    if h == 3:
                        e_stage(0)
                    vcol, gcol = VB + h * 256, GB + h * 256
                slab, sb = load_slab([(w_in[:, vcol: vcol + 256], 256, 0)], KT)
                proj_tm(slab, sb, tiles, lambda p, bk, ti, n: act(VS[hb][0:n, ti, :], p, AF.Copy, reads=[PB[bk]], writes=[B_VS[hb]]))
                if not prefix:
                    slab, sb = load_slab([(w_in[:, gcol: gcol + 256], 256, 0)], KT)
                    proj_tm(slab, sb, tiles, silu_evac(lambda ti, n: S2[hb][0:n, ti, :], B_S2[hb]))
                drain()
                pending.append(attention(br, h, hb, prefix, elast, ebuf))
                if not prefix:
                    pending.append(sample_step(br, h, hb, egs, B_EGS[h % 3]))
        if final_drain:
            drain()

    def barrier_note():
        pass

    norm_transpose(xp, PRE_TILES, C_NMIX)
    win_pass(True, final_drain=False)
    norm_transpose(xo, OWN_TILES, C_NMIX, ticker=lambda: tick())
    drain()
    win_pass(False)
    T.barrier()

    gemm_banks[:] = [0, 1, 2, 3, 4, 5, 6, 7]
    MGT = [Fv(O_MG + i * 2048, 512) for i in range(5)]
    B_MGT = [Buf() for _ in range(5)]

    def sigmoid_to(dst, dbuf, p, bk, n):
        act(dst[:, 0:n], p, AF.Exp, reads=[PB[bk]], writes=[dbuf], scale=-1.0)
        act(dst[:, 0:n], dst[:, 0:n], AF.Ln, reads=[dbuf], writes=[dbuf], bias=1.0)
        act(dst[:, 0:n], dst[:, 0:n], AF.Exp, reads=[dbuf], writes=[dbuf], scale=-1.0)

    X1 = [Fv(O_X1 + i * 8192, D) for i in range(9)]
    B_X1 = [Buf() for _ in range(9)]
    X1_EARLY = (2, 3, 4, 5)
    assert O_X1 + 2 * 8192 >= O_MG + 5 * 2048 and O_X1 + 6 * 8192 <= O_OT
    for ti in X1_EARLY:
        t0, t1 = OWN_TILES[ti]
        T.dma("sp", X1[ti][0:t1 - t0, :], xo[t0:t1, :], x1_s[ti], writes=[B_X1[ti]])
    switch_mode(8)
    for nt in range(16):
        c0 = nt * 128
        sab, sabb = load_slab([(w_aup[:, c0:c0 + 128], 128, 0), (w_bup[:, c0:c0 + 128], 128, 128)], 8)
        sga, sgab = load_slab([(w_in[:, MG + c0: MG + c0 + 128], 128, 0)], KT)
        sgb, sgbb = load_slab([(w_in[:, MG + D + c0: MG + D + c0 + 128], 128, 0)], KT)
        for (t0, t1) in OWN_BLOCKS:
            n = t1 - t0
            ba, bb, bga, bgb = gb(), gb(), gb(), gb()
            for kt in range(KT):
                mm(ps[bga][:, 0:n], sga[:, kt, :], HT[:, kt, t0:t1], kt == 0, kt == KT - 1, [sgab, B_HT], bga, kt == KT - 1)
            for kt in range(KT):
                mm(ps[bgb][:, 0:n], sgb[:, kt, :], HT[:, kt, t0:t1], kt == 0, kt == KT - 1, [sgbb, B_HT], bgb, kt == KT - 1)
            for kt in range(8):
                mm(ps[ba][:, 0:n], sab[:, kt, 0:128], OT[:, kt, t0:t1], kt == 0, kt == 7, [sabb, B_OT], ba, kt == 7)
            for kt in range(8):
                mm(ps[bb][:, 0:n], sab[:, kt, 128:256], OT[:, 8 + kt, t0:t1], kt == 0, kt == 7, [sabb, B_OT], bb, kt == 7)
            sigmoid_to(MGT[0], B_MGT[0], ps[bga][:, 0:n], bga, n)
            sigmoid_to(MGT[1], B_MGT[1], ps[bgb][:, 0:n], bgb, n)
            tt(MGT[2][:, 0:n], ps[ba][:, 0:n], MGT[0][:, 0:n], ALU.mult, [PB[ba], B_MGT[0]], [B_MGT[2]])
            tt(MGT[3][:, 0:n], ps[bb][:, 0:n], MGT[1][:, 0:n], ALU.mult, [PB[bb], B_MGT[1]], [B_MGT[3]])
            tt(MT[:, nt, t0:t1], MGT[2][:, 0:n], MGT[3][:, 0:n], ALU.add, [B_MGT[2], B_MGT[3]], [B_MT])
    switch_mode(4)

    T.barrier()
    for ti, (t0, t1) in enumerate(OWN_TILES):
        if ti not in X1_EARLY:
            T.dma("sp", X1[ti][0:t1 - t0, :], xo[t0:t1, :], x1_s[ti], writes=[B_X1[ti]])
    def wout_group(slab, sb, cg, ti):
        t0, t1 = OWN_TILES[ti]
        n = t1 - t0
        bk = gb()
        for kt in range(KT):
            mm(ps[bk][0:n, 0:256], MT[:, kt, t0:t1], slab[:, kt, :], kt == 0, kt == KT - 1, [B_MT, sb], bk, kt == KT - 1)
        xs_ = X1[ti][0:n, cg * 256:(cg + 1) * 256]
        tt(xs_, xs_, ps[bk][0:n, 0:256], ALU.add, [PB[bk], B_X1[ti]], [B_X1[ti]])

    for cg in range(4):
        slab, sb = load_slab([(w_out[:, cg * 256:(cg + 1) * 256], 256, 0)], KT)
        for ti in (2, 3, 4, 5, 0, 1, 6, 7, 8):
            wout_group(slab, sb, cg, ti)
    slabs2 = [load_slab([(w_out[:, cg * 256:(cg + 1) * 256], 256, 0)], KT) for cg in range(4, 8)]
    norm2_tile = norm_transpose(None, OWN_TILES, C_NFFN, from_x1=[(X1[i], B_X1[i]) for i in range(9)], only_setup=True)
    for ti in range(9):
        for j, cg in enumerate(range(4, 8)):
            wout_group(slabs2[j][0], slabs2[j][1], cg, ti)
        if ti >= 1:
            norm2_tile(ti - 1)
    norm2_tile(8)

    ACTT = MT
    B_ACTT = B_MT
    FT = [Fv(O_LATE + i * 2048, 512) for i in range(4)]
    B_FT = [Buf() for _ in range(4)]
    fr = [0]
    for blk in range(4):
        r0 = blk * 1408
        for j in range(6):
            ncols = 256 if j < 5 else 128
            c0 = r0 + j * 256
            sgt, sgtb = load_slab([(w_fg[:, c0:c0 + ncols], ncols, 0)], KT)
            sut, sutb = load_slab([(w_fu[:, c0:c0 + ncols], ncols, 0)], KT)
            for q in range(ncols // 128):
                jt = j * 2 + q
                for (t0, t1) in OWN_BLOCKS:
                    n = t1 - t0
                    ba, bb = gb(), gb()
                    for kt in range(KT):
                        mm(ps[ba][:, 0:n], sgt[:, kt, q * 128:(q + 1) * 128], HT[:, kt, t0:t1], kt == 0, kt == KT - 1, [sgtb, B_HT], ba, kt == KT - 1)
                    for kt in range(KT):
                        mm(ps[bb][:, 0:n], sut[:, kt, q * 128:(q + 1) * 128], HT[:, kt, t0:t1], kt == 0, kt == KT - 1, [sutb, B_HT], bb, kt == KT - 1)
                    i = fr[0] % 2
                    fr[0] += 1
                    sigmoid_to(FT[i], B_FT[i], ps[ba][:, 0:n], ba, n)
                    tt(FT[2 + i][:, 0:n], ps[ba][:, 0:n], FT[i][:, 0:n], ALU.mult, [PB[ba], B_FT[i]], [B_FT[2 + i]])
                    tt(ACTT[:, jt, t0:t1], FT[2 + i][:, 0:n], ps[bb][:, 0:n], ALU.mult, [B_FT[2 + i], PB[bb]], [B_ACTT])
        def down_group(slab, sb, cg, ti):
            t0, t1 = OWN_TILES[ti]
            n = t1 - t0
            bk = gb()
            for kt in range(11):
                mm(ps[bk][0:n, 0:256], ACTT[:, kt, t0:t1], slab[:, kt, :], kt == 0, kt == 10, [B_ACTT, sb], bk, kt == 10)
            xs_ = X1[ti][0:n, cg * 256:(cg + 1) * 256]
            tt(xs_, xs_, ps[bk][0:n, 0:256], ALU.add, [PB[bk], B_X1[ti]], [B_X1[ti]])

        for cg in range(8 if blk < 3 else 4):
            slab, sb = load_slab([(w_fd[r0:r0 + 1408, cg * 256:(cg + 1) * 256], 256, 0)], 11)
            for ti in range(9):
                down_group(slab, sb, cg, ti)

    r0 = 3 * 1408
    slabs3 = [load_slab([(w_fd[r0:r0 + 1408, cg * 256:(cg + 1) * 256], 256, 0)], 11) for cg in range(4, 8)]
    NFB = Fv(O_LATE, D); B_NFB = Buf()
    T.dma("sp", NFB, nfb, misc3_s, reads=B_FT, writes=[B_NFB] + B_FT)
    JN = Hv(O_LATE + 8192, D)

    def final_tile(ti):
        t0, t1 = OWN_TILES[ti]
        n = t1 - t0
        sc, b = sml()
        act(JN[0:n, :], X1[ti][0:n, :], AF.Square, reads=[B_X1[ti]], writes=[B_XN2[0], b], accum=sc[0:n, 2:3])
        r, rb = rstd_from_ss(sc[0:n, 2:3], D, b, n)
        stt(X1[ti][0:n, :], X1[ti][0:n, :], r, NFB[0:n, :], ALU.mult, ALU.mult, [B_X1[ti], rb, B_NFB], [B_X1[ti]])
        T.dma("sp", yo[t0:t1, :], X1[ti][0:n, :], out_s, reads=[B_X1[ti]])

    for ti in range(9):
        for j, cg in enumerate(range(4, 8)):
            down_group(slabs3[j][0], slabs3[j][1], cg, ti)
        if ti >= 1:
            final_tile(ti - 1)
    final_tile(8)

    final_waits = [(out_s, T.dsem[out_s][1])] + [(s, T.dsem[s][1]) for s in sn_s]
    block = es.enter_context(nc.Block())
    T.emit(block, final_waits)
    es.close()
    return nc


_CACHE = {}


def _consts():
    ident = np.eye(128, dtype=np.float32)
    jj = np.arange(128)
    cmask = (jj[:, None] <= jj[None, :]).astype(np.float32)
    dq = np.zeros((128, 4, 128), np.float32)
    dk = np.zeros((128, 4, 128), np.float32)
    for h in range(4):
        dq[:, h, :] = np.exp((jj + 1) * _LG[h])[None, :]
        dk[:, h, :] = (np.exp(-(jj + 1) * _LG[h]) / 16.0)[None, :]
    return ident, cmask, dq.reshape(128, 512), dk.reshape(128, 512)


def _rope(pos):
    ang = (pos.astype(np.float32)[None, :] * _INV[:, None]).astype(np.float32)
    ang = ang.astype(np.float64)
    return np.stack([np.cos(ang), np.sin(ang)], axis=1).astype(np.float32)


def kernel(x_prompt, x_sample, state_gla, state_ret, norm_mix, w_in, w_gla_gate_up, b_gla_gate,
           gla_norm_w, w_gla_up, ret_norm_w, w_ret_up, w_out, norm_ffn, w_ffn_gate, w_ffn_up,
           w_ffn_down, norm_final):
    f = lambda a: np.ascontiguousarray(np.asarray(a, dtype=np.float32))
    x_prompt = f(x_prompt); x_sample = f(x_sample); state_gla = f(state_gla); state_ret = f(state_ret)
    if "nc" not in _CACHE:
        _CACHE["nc"] = build_program()
    nc = _CACHE["nc"]
    ident, cmask, dq, dk = _consts()
    col = lambda v: np.ascontiguousarray(f(v).reshape(-1, 128).T)
    cst = np.zeros((128, NCST), np.float32)
    cst[:, C_ID:C_ID + 128] = ident
    cst[:, C_CM:C_CM + 128] = cmask
    cst[:, C_DQ:C_DQ + 512] = dq
    cst[:, C_DK:C_DK + 512] = dk
    cst[:, C_NMIX:C_NMIX + 16] = col(norm_mix[0])
    cst[:, C_NFFN:C_NFFN + 16] = col(norm_ffn[0])
    cst[:, C_BG:C_BG + 4] = col(b_gla_gate[0])
    cst[:, C_GANW:C_GANW + 2] = col(gla_norm_w[0])
    cst[:, C_RENW:C_RENW + 2] = col(ret_norm_w[0])
    nfb = np.ascontiguousarray(np.broadcast_to(f(norm_final)[None, :], (128, D)))
    shared = {
        "w_in": f(w_in[0]), "w_gup": f(w_gla_gate_up[0]), "w_aup": f(w_gla_up[0]), "w_bup": f(w_ret_up[0]),
        "w_out": f(w_out[0]), "w_fg": f(w_ffn_gate[0]), "w_fu": f(w_ffn_up[0]), "w_fd": f(w_ffn_down[0]),
        "cst": cst, "nfb": nfb,
    }
    in_maps = []
    zeros_pre = np.zeros((T_PRE, D), np.float32)
    for c in range(8):
        b, half = c // 2, c % 2
        xo = np.concatenate([x_prompt[b, half * 1024:(half + 1) * 1024], x_sample[c * NS:(c + 1) * NS, 0]], axis=0)
        xp = x_prompt[b, 0:1024] if half == 1 else zeros_pre
        pos_o = np.concatenate([np.arange(half * 1024, (half + 1) * 1024), np.full(NS, 16384)])
        m = dict(shared)
        m.update({"xo": np.ascontiguousarray(xo), "xp": np.ascontiguousarray(xp),
                  "sg": np.ascontiguousarray(state_gla[0, c * NS:(c + 1) * NS]),
                  "sr": np.ascontiguousarray(state_ret[0, c * NS:(c + 1) * NS]),
                  "cso": _rope(pos_o), "csp": _rope(np.arange(0, 1024))})
        in_maps.append(m)
    res = run_bass_kernel_spmd(nc, in_maps, core_ids=list(range(8)))
    R = res.results
    y_prompt = np.zeros((4, 2048, D), np.float32)
    y_sample = np.zeros((128, 1, D), np.float32)
    gla_p = np.zeros((1, 4, 4, 128, 256), np.float32)
    ret_p = np.zeros((1, 4, 4, 256, 256), np.float32)
    gla_s = np.zeros((1, 128, 4, 128, 256), np.float32)
    ret_s = np.zeros((1, 128, 4, 256, 256), np.float32)
    for c in range(8):
        b, half = c // 2, c % 2
        y_prompt[b, half * 1024:(half + 1) * 1024] = R[c]["yo"][0:1024]
        y_sample[c * NS:(c + 1) * NS, 0] = R[c]["yo"][1024:1040]
        if half == 1:
            gla_p[0, b] = R[c]["gp"]
            ret_p[0, b] = R[c]["rp"]
        gla_s[0, c * NS:(c + 1) * NS] = R[c]["gs"]
        ret_s[0, c * NS:(c + 1) * NS] = R[c]["rs"]
    return (y_prompt, y_sample, gla_p, ret_p, gla_s, ret_s)
```
